# Optimizing a Trainium2 kernel written in Bass

```python
import math
import jax, jax.numpy as jnp
from jax import lax
import numpy as np

D_MODEL = 1024
BATCH = 8
SEQ = 2048
DEPTH = 1
DEC_BATCH = 32
DEC_SEQ = 4
PAST_LEN = 16384
PAGE_SIZE = 128

HG_HEADS = 4
HG_DK = 128
HG_DV = 128
HG_CHUNK = 64
MLA_HEADS = 4
QK_NOPE = 128
QK_ROPE = 64
V_DIM = 128
Q_LORA = 768
KV_LORA = 256
ROPE_THETA = 10000.0
Q_BLOCK = 128
D_FF = 2816
EPS = 1e-6

HG_KW = HG_HEADS * HG_DK
HG_WIDTH = HG_HEADS * HG_DV
MLA_WIDTH = MLA_HEADS * V_DIM
MIX_WIDTH = HG_WIDTH + MLA_WIDTH
IN_SIZES = (HG_KW, HG_KW, HG_WIDTH, HG_WIDTH, Q_LORA, KV_LORA, QK_ROPE)
IN_WIDTH = sum(IN_SIZES)
IN_OFFSETS = [int(v) for v in np.cumsum(IN_SIZES)[:-1]]
N_NORMS = 6

kernel_name = "hymba_hgrn2_mla_macaron_step"


def rmsnorm(x, g):
    xf = x.astype(jnp.float32)
    y = xf * lax.rsqrt(jnp.mean(xf * xf, axis=-1, keepdims=True) + EPS)
    return (y * g.astype(jnp.float32)).astype(x.dtype)


def rope(x, pos):
    half = x.shape[-1] // 2
    inv = ROPE_THETA ** (-jnp.arange(half, dtype=jnp.float32) / half)
    ang = pos.astype(jnp.float32)[:, None] * inv[None, :]
    cos = jnp.cos(ang)[:, None, :]
    sin = jnp.sin(ang)[:, None, :]
    xf = x.astype(jnp.float32)
    x1, x2 = xf[..., :half], xf[..., half:]
    out = jnp.concatenate([x1 * cos - x2 * sin, x1 * sin + x2 * cos], axis=-1)
    return out.astype(x.dtype)


def swiglu(x, wg, wu, wd):
    return (jax.nn.silu(x @ wg) * (x @ wu)) @ wd


def hgrn2_chunked(q, k, v, logf, s0):
    B, L, H, DK = q.shape
    DV = v.shape[-1]
    C = math.gcd(L, HG_CHUNK)
    n = L // C

    def chunks(a):
        return a.astype(jnp.float32).reshape(B, n, C, H, a.shape[-1]).transpose(1, 0, 3, 2, 4)

    tril = jnp.tril(jnp.ones((C, C), dtype=bool))

    def step(S, inp):
        qc, kc, vc, gc = inp
        b = jnp.cumsum(gc, axis=-2)
        diff = b[..., :, None, :] - b[..., None, :, :]
        decay = jnp.exp(jnp.where(tril[:, :, None], diff, -jnp.inf))
        A = jnp.einsum('bhtsk,bhsk->bhts', decay * qc[..., :, None, :], kc)
        o = jnp.einsum('bhts,bhsv->bhtv', A, vc) + jnp.einsum('bhtk,bhkv->bhtv', qc * jnp.exp(b), S)
        bl = b[..., -1:, :]
        S_new = jnp.exp(bl[..., 0, :])[..., None] * S + jnp.einsum('bhsk,bhsv->bhkv', kc * jnp.exp(bl - b), vc)
        return S_new, o

    S_fin, o = lax.scan(step, s0.astype(jnp.float32), (chunks(q), chunks(k), chunks(v), chunks(logf)))
    o = o.transpose(1, 0, 3, 2, 4).reshape(B, L, H, DV)
    return o.astype(v.dtype), S_fin.astype(s0.dtype)


def mla_attend(q_lat, q_rope, c_kv, k_rope, q_pos, k_pos):
    B, T = q_lat.shape[:2]
    blk = math.gcd(T, Q_BLOCK)
    nb = T // blk
    scale = 1.0 / math.sqrt(QK_NOPE + QK_ROPE)

    def blocks(a):
        return jnp.moveaxis(a.reshape(B, nb, blk, *a.shape[2:]), 1, 0)

    def one(args):
        ql, qr, qp = args
        s = (jnp.einsum('bqhr,bsr->bhqs', ql, c_kv) + jnp.einsum('bqhd,bsd->bhqs', qr, k_rope)).astype(jnp.float32) * scale
        s = jnp.where(k_pos[None, None, None, :] <= qp[None, None, :, None], s, -jnp.inf)
        p = jax.nn.softmax(s, axis=-1).astype(c_kv.dtype)
        return jnp.einsum('bhqs,bsr->bqhr', p, c_kv)

    out = lax.map(one, (blocks(q_lat), blocks(q_rope), q_pos.reshape(nb, blk)))
    return jnp.moveaxis(out, 0, 1).reshape(q_lat.shape)


def mixer(u, pos, past_ckv, past_kr, past_pos, s0, lb, w_in, q_norm_g, w_q_up, kv_norm_g, w_kv_up, hg_norm_g, w_out):
    B, T, _ = u.shape
    z = u @ w_in
    hq, hf, hi, hg, cq, ckv, kr = jnp.split(z, IN_OFFSETS, axis=-1)
    f = lb + (1.0 - lb) * jax.nn.sigmoid(hf.astype(jnp.float32))
    logf = jnp.log(f)
    k = 1.0 - f
    q = jax.nn.silu(hq)
    heads = lambda a, d: a.reshape(B, T, HG_HEADS, d)
    o_h, s_fin = hgrn2_chunked(heads(q, HG_DK), heads(k, HG_DK), heads(hi, HG_DV), heads(logf, HG_DK), s0)
    o_h = rmsnorm(o_h, hg_norm_g) * jax.nn.silu(heads(hg, HG_DV))
    cq = rmsnorm(cq, q_norm_g)
    qh = jnp.einsum('btc,chd->bthd', cq, w_q_up)
    q_nope = qh[..., :QK_NOPE]
    q_r = rope(qh[..., QK_NOPE:], pos)
    c_kv = rmsnorm(ckv, kv_norm_g)
    k_r = rope(kr[:, :, None, :], pos)[:, :, 0, :]
    w_uk = w_kv_up[..., :QK_NOPE]
    w_uv = w_kv_up[..., QK_NOPE:]
    q_lat = jnp.einsum('bthd,rhd->bthr', q_nope, w_uk)
    keys_c = jnp.concatenate([past_ckv, c_kv], axis=1)
    keys_r = jnp.concatenate([past_kr, k_r], axis=1)
    k_pos = jnp.concatenate([past_pos, pos])
    o_lat = mla_attend(q_lat, q_r, keys_c, keys_r, pos, k_pos)
    o_m = jnp.einsum('bthr,rhv->bthv', o_lat, w_uv)
    o = jnp.concatenate([o_h.reshape(B, T, HG_WIDTH), o_m.reshape(B, T, MLA_WIDTH)], axis=-1)
    return o @ w_out, c_kv, k_r, s_fin


def layer(h, pos, past_ckv, past_kr, past_pos, s0, lb, norm_g, wg, wu, wd,
          w_in, q_norm_g, w_q_up, kv_norm_g, w_kv_up, hg_norm_g, w_out):
    h = h + 0.5 * rmsnorm(swiglu(rmsnorm(h, norm_g[0]), wg[0], wu[0], wd[0]), norm_g[1])
    m, c_kv, k_r, s_fin = mixer(rmsnorm(h, norm_g[2]), pos, past_ckv, past_kr, past_pos, s0, lb,
                                w_in, q_norm_g, w_q_up, kv_norm_g, w_kv_up, hg_norm_g, w_out)
    h = h + rmsnorm(m, norm_g[3])
    h = h + 0.5 * rmsnorm(swiglu(rmsnorm(h, norm_g[4]), wg[1], wu[1], wd[1]), norm_g[5])
    return h, c_kv, k_r, s_fin


def setup_inputs(seed: int = 0) -> dict:
    key = jax.random.key(seed)
    ks = jax.random.split(key, 20)
    n_pages = PAST_LEN // PAGE_SIZE
    n_used = DEC_BATCH * n_pages
    n_pool = (5 * n_used) // 4
    nrm = lambda k, shape, s: jax.random.normal(k, shape, dtype=jnp.float32) * s
    page_table = jax.random.permutation(ks[5], n_pool)[:n_used].reshape(DEC_BATCH, n_pages).astype(jnp.int32)
    return {
        "x_prompt": nrm(ks[0], (BATCH, SEQ, D_MODEL), 1.0),
        "x_sample": nrm(ks[1], (DEC_BATCH, DEC_SEQ, D_MODEL), 1.0),
        "cache_kv_latent": nrm(ks[2], (DEPTH, n_pool, PAGE_SIZE, KV_LORA), 1.0),
        "cache_k_rope": nrm(ks[3], (DEPTH, n_pool, PAGE_SIZE, QK_ROPE), 1.0),
        "state_hgrn": nrm(ks[4], (DEPTH, DEC_BATCH, HG_HEADS, HG_DK, HG_DV), 0.5),
        "page_table": page_table,
        "hgrn_lb_logits": nrm(ks[6], (DEPTH + 1, HG_KW), 0.5),
        "norm_g": 1.0 + nrm(ks[7], (DEPTH, N_NORMS, D_MODEL), 0.1),
        "w_ffn_gate": nrm(ks[8], (DEPTH, 2, D_MODEL, D_FF), D_MODEL ** -0.5),
        "w_ffn_up": nrm(ks[9], (DEPTH, 2, D_MODEL, D_FF), D_MODEL ** -0.5),
        "w_ffn_down": nrm(ks[10], (DEPTH, 2, D_FF, D_MODEL), D_FF ** -0.5),
        "w_in": nrm(ks[11], (DEPTH, D_MODEL, IN_WIDTH), D_MODEL ** -0.5),
        "q_norm_g": 1.0 + nrm(ks[12], (DEPTH, Q_LORA), 0.1),
        "w_q_up": nrm(ks[13], (DEPTH, Q_LORA, MLA_HEADS, QK_NOPE + QK_ROPE), Q_LORA ** -0.5),
        "kv_norm_g": 1.0 + nrm(ks[14], (DEPTH, KV_LORA), 0.1),
        "w_kv_up": nrm(ks[15], (DEPTH, KV_LORA, MLA_HEADS, QK_NOPE + V_DIM), KV_LORA ** -0.5),
        "hg_norm_g": 1.0 + nrm(ks[16], (DEPTH, HG_DV), 0.1),
        "w_out": nrm(ks[17], (DEPTH, MIX_WIDTH, D_MODEL), MIX_WIDTH ** -0.5),
    }


def reference(x_prompt, x_sample, cache_kv_latent, cache_k_rope, state_hgrn, page_table,
              hgrn_lb_logits, norm_g, w_ffn_gate, w_ffn_up, w_ffn_down, w_in, q_norm_g, w_q_up,
              kv_norm_g, w_kv_up, hg_norm_g, w_out):
    Bp, Tp, _ = x_prompt.shape
    Bs, Ts, _ = x_sample.shape
    past_len = page_table.shape[1] * cache_kv_latent.shape[2]
    pos_p = jnp.arange(Tp, dtype=jnp.int32)
    pos_s = past_len + jnp.arange(Ts, dtype=jnp.int32)
    past_pos_s = jnp.arange(past_len, dtype=jnp.int32)
    empty_pos = jnp.zeros((0,), dtype=jnp.int32)
    lb_all = jnp.cumsum(jax.nn.softmax(hgrn_lb_logits.astype(jnp.float32), axis=0), axis=0)

    hp, hs = x_prompt, x_sample
    ckv_p, kr_p, st_p, ckv_s, kr_s, st_s = [], [], [], [], [], []
    for l in range(DEPTH):
        w = (lb_all[l], norm_g[l], w_ffn_gate[l], w_ffn_up[l], w_ffn_down[l], w_in[l], q_norm_g[l],
             w_q_up[l], kv_norm_g[l], w_kv_up[l], hg_norm_g[l], w_out[l])
        hp, c1, r1, s1 = layer(hp, pos_p,
                               jnp.zeros((Bp, 0, KV_LORA), hp.dtype), jnp.zeros((Bp, 0, QK_ROPE), hp.dtype),
                               empty_pos, jnp.zeros((Bp, HG_HEADS, HG_DK, HG_DV), hp.dtype), *w)
        past_c = cache_kv_latent[l][page_table].reshape(Bs, past_len, KV_LORA)
        past_r = cache_k_rope[l][page_table].reshape(Bs, past_len, QK_ROPE)
        hs, c2, r2, s2 = layer(hs, pos_s, past_c, past_r, past_pos_s, state_hgrn[l], *w)
        ckv_p.append(c1); kr_p.append(r1); st_p.append(s1)
        ckv_s.append(c2); kr_s.append(r2); st_s.append(s2)

    return (hp, hs, jnp.stack(ckv_p), jnp.stack(kr_p), jnp.stack(st_p),
            jnp.stack(ckv_s), jnp.stack(kr_s), jnp.stack(st_s))
```

```python
import math
from contextlib import ExitStack
import numpy as np
import concourse.bass as bass
import concourse.mybir as mybir
from concourse.bass_utils import run_bass_kernel_spmd

F32 = mybir.dt.float32
BF16 = mybir.dt.bfloat16
I32 = mybir.dt.int32
AF = mybir.ActivationFunctionType
ALU = mybir.AluOpType

NCORES = 8
D = 1024
DFF = 2816
SEQ = 2048
DEC_B = 32
DEC_T = 4
PAGE = 128
NPAGES = 128
KVL = 256
ROPE = 64
EPS = 1e-6
SCALE = 1.0 / math.sqrt(192.0)
SEM_CAP = 8000
SLOT = 4096
NSLOT = 5
WMAX = 512
CACHE_ROWS = 5120 * 8
SW = 528


class Sched:
    def __init__(self, nc, stack):
        self.nc = nc
        self.stack = stack
        self.engs = {'pe': nc.tensor, 'act': nc.scalar, 'dve': nc.vector, 'pool': nc.gpsimd, 'sp': nc.sync}
        self.ops = {e: [] for e in self.engs}
        self.cnt = {e: 0 for e in self.engs}
        self.cursem = {}
        self.nsem = 0
        for e in self.engs:
            self.cursem[e] = self._newsem()
        self.waited = {e: {} for e in self.engs}
        self.lastw = {}
        self.readers = {}
        self.dma_cum = {}
        self.dma_sems = {}

    def _newsem(self):
        self.nsem += 1
        return self.stack.enter_context(self.nc.semaphore(f"s{self.nsem}"))

    def op(self, eng, fn, reads=(), writes=(), dma=None):
        deps = {}

        def add(tok):
            if tok is None:
                return
            sem, val, src, sid = tok
            if eng == 'pe' and src == 'pe':
                return
            if self.waited[eng].get(sid, 0) >= val:
                return
            if sid not in deps or deps[sid][1] < val:
                deps[sid] = (sem, val)
        for r in reads:
            add(self.lastw.get(r))
        for w in writes:
            add(self.lastw.get(w))
            for t in self.readers.get(w, ()):
                add(t)
        waits = []
        for sid, (sem, val) in deps.items():
            self.waited[eng][sid] = val
            waits.append((sem, val))
        if dma is not None:
            if dma not in self.dma_sems:
                self.dma_sems[dma] = self._newsem()
                self.dma_cum[dma] = 0
            self.dma_cum[dma] += 16
            tok = (self.dma_sems[dma], self.dma_cum[dma], 'dma', ('d', dma))
            inc = 16
        else:
            if self.cnt[eng] >= SEM_CAP:
                self.cursem[eng] = self._newsem()
                self.cnt[eng] = 0
            self.cnt[eng] += 1
            tok = (self.cursem[eng], self.cnt[eng], eng, id(self.cursem[eng]))
            inc = 1
        for w in writes:
            self.lastw[w] = tok
            self.readers[w] = []
        for r in reads:
            if r not in writes:
                self.readers.setdefault(r, []).append(tok)
        self.ops[eng].append((waits, fn, tok[0], inc))
        return tok

    def wait_tokens(self, eng, toks):
        waits = []
        for tok in toks:
            sem, val, src, sid = tok
            if self.waited[eng].get(sid, 0) >= val:
                continue
            self.waited[eng][sid] = val
            waits.append((sem, val))
        self.ops[eng].append((waits, None, None, 0))

    def emit(self):
        nc = self.nc
        with nc.Block() as block:
            def run(name):
                def body(e):
                    for waits, fn, sem, inc in self.ops[name]:
                        for (s, v) in waits:
                            e.wait_ge(s, v)
                        if fn is not None:
                            fn(e).then_inc(sem, inc)
                return body
            block.sync(run('sp'))
            block.scalar(run('act'))
            block.vector(run('dve'))
            block.gpsimd(run('pool'))
            block.tensor(run('pe'))


def bc(ap, dims):
    return bass.AP(ap.tensor, ap.offset, [list(ap.ap[0])] + [list(d) for d in dims])


def build(stages=99, debug=False):
    nc = bass.Bass("TRN2", target_bir_lowering=False)
    di = lambda name, shape, dt=F32: nc.dram_tensor(name, list(shape), dt, kind="ExternalInput").ap()
    do = lambda name, shape, dt=F32: nc.dram_tensor(name, list(shape), dt, kind="ExternalOutput").ap()
    XP = di("xp", [SEQ, D]); XS = di("xs", [16, D])
    CKV = di("cache_kv", [CACHE_ROWS, 16 * KVL]); CKR = di("cache_kr", [CACHE_ROWS, 16 * ROPE])
    ST0 = di("st0", [128, 16, 128]); PT = di("pt", [128, 4], I32)
    VEC = di("vec", [128, 72]); WGU = di("wgu", [2, 128, 11, 4096]); WD = di("wd", [2, 128, 8, 2816])
    WIN = di("win", [128, 8, 4096]); WQ = di("wq", [128, 2, 3072]); WUKV = di("wukv", [128, 2048])
    WO = di("wo", [128, 2, 4096]); IDENT = di("ident", [128, 128]); MASKP = di("maskp", [128, 2048])
    MASKH = di("maskh", [64, 64]); RC = di("ropec", [64, SEQ + 16]); RS = di("ropes", [64, SEQ + 16])
    YP = do("yp", [SEQ, D]); YS = do("ys", [16, D]); KVP = do("kvp", [SEQ, KVL]); KRP = do("krp", [SEQ, ROPE])
    STP = do("stp", [4, 128, 128]); KVS = do("kvs", [16, KVL]); KRS = do("krs", [16, ROPE]); STS = do("sts", [16, 128, 128])
    DBG = do("dbg", [128, 8, WMAX]) if debug else None

    with ExitStack() as st:
        S = Sched(nc, st)
        sb = lambda name, shape, dt: st.enter_context(nc.sbuf_tensor("sb_" + name, list(shape), dt))
        PS = [st.enter_context(nc.psum_tensor(f"ps{i}", [128, 512], F32)) for i in range(8)]
        PSB = [p[:].bitcast(BF16) for p in PS]
        pk = lambda i: ('ps', i)
        hT = sb("hT", [128, 8, WMAX], F32)
        xn = sb("xn", [128, 8, WMAX], BF16)
        yb = sb("yb", [128, 8, WMAX], F32)
        scr = sb("scr", [128, 11, SW], F32)
        ring = sb("ring", [128, NSLOT, SLOT], BF16)
        sq = sb("sq", [128, 2, WMAX], BF16)
        sg = sb("sg", [128, 2, WMAX], F32)
        rstd = sb("rstd", [128, WMAX], F32)
        ckvT = sb("ckvT", [128, 2, SEQ], BF16)
        krT = sb("krT", [128, SEQ], BF16)
        ckvtok = sb("ckvtok", [128, 16, KVL], BF16)
        ckvtoks = sb("ckvtoks", [4, 4, KVL], BF16)
        Sf = sb("Sf", [128, 20, 128], F32)
        Sb = sb("Sb", [128, 20, 128], BF16)
        vec = sb("vec", [128, 72], F32)
        identf = sb("identf", [128, 128], F32)
        identb = sb("identb", [128, 128], BF16)
        onesb = sb("onesb", [128, 128], BF16)
        onesf = sb("onesf", [128, 1], F32)
        maskp = sb("maskp", [128, 2048], BF16)
        maskh = sb("maskh", [64, 64], F32)
        ropec = sb("ropec", [64, WMAX], F32)
        ropes = sb("ropes", [64, WMAX], F32)
        wukv = sb("wukv", [128, 2048], BF16)
        vtok = sb("vtok", [64, 8, 512], BF16)
        ktok = sb("ktok", [64, 8, 128], BF16)
        qn = sb("qn", [128, 2, WMAX], BF16)
        qlat = sb("qlat", [128, 4, 2, WMAX], BF16)
        qr = sb("qr", [128, 4, WMAX], BF16)
        krf = sb("krf", [64, WMAX], F32)
        pT = sb("pT", [128, 3, WMAX], BF16)
        on = sb("on", [128, 2, WMAX], BF16)
        ocat = sb("ocat", [128, 8, WMAX], BF16)
        kvst = sb("kvst", [128, 4, KVL], F32)
        krst = sb("krst", [128, 4, ROPE], F32)
        cst4 = sb("cst4", [128, 4, 3, 8], F32)
        epst = sb("epst", [128, 1], F32)
        pti = sb("pti", [128, 4], I32)
        idx = sb("idx", [128, 4], I32)
        NG = lambda n, c: vec[:, n * 8 + c: n * 8 + c + 1]
        QG = lambda c: vec[:, 48 + c: 49 + c]
        KG = lambda c: vec[:, 54 + c: 55 + c]
        HG = vec[:, 56:57]
        lbt = sb("lbt", [128, 16], F32)
        g05 = sb("g05", [128, 16], F32)

        def dma(eng, out, in_, reads, writes, key):
            return S.op(eng, lambda e: e.dma_start(out=out, in_=in_), reads=reads, writes=writes, dma=key)
        dma('sp', vec[:], VEC, [], ['vec'], 'vec')
        dma('sp', identf[:], IDENT, [], ['identf'], 'identf')
        dma('sp', maskh[:], MASKH, [], ['maskh'], 'maskh')
        dma('sp', pti[:], PT, [], ['pti'], 'pti')
        dma('sp', Sf[:, 4:20, :], ST0, [], [('Sf', i) for i in range(4, 20)], 'st0')
        dma('pool', maskp[:].rearrange("p (a b) -> p a b", a=4), MASKP.rearrange("p (a b) -> p a b", a=4), [], ['maskp', 'maskp_hi'], 'maskp')
        dma('pool', wukv[:].rearrange("p (a b) -> p a b", a=2), WUKV.rearrange("p (a b) -> p a b", a=2), [], ['wukv'], 'wukv')
        S.op('dve', lambda e: e.tensor_copy(out=identb[:], in_=identf[:]), ['identf'], ['identb'])
        S.op('dve', lambda e: e.memset(onesb[:], 1.0), [], ['onesb'])
        S.op('dve', lambda e: e.memset(onesf[:], 1.0), [], ['onesf'])
        S.op('dve', lambda e: e.memset(epst[:], EPS), [], ['epst'])
        S.op('dve', lambda e: e.memset(krT[:], 0.0), [], [('krT', i) for i in range(4)])
        S.op('dve', lambda e: e.memset(qr[:], 0.0), [], [('qr', h) for h in range(4)])
        S.op('dve', lambda e: e.memset(Sf[:, 0:4, :], 0.0), [], [('Sf', i) for i in range(4)])
        S.op('dve', lambda e: e.memset(Sb[:, 0:4, :], 0.0), [], [('Sb', i) for i in range(4)])
        S.op('act', lambda e: e.activation(out=Sb[:, 4:20, :], in_=Sf[:, 4:20, :], func=AF.Copy),
             [('Sf', i) for i in range(4, 20)], [('Sb', i) for i in range(4, 20)])
        S.op('dve', lambda e: e.tensor_tensor(out=lbt[:, 8:12], in0=vec[:, 57:61], in1=vec[:, 61:65], op=ALU.subtract), ['vec'], ['lbt'])
        S.op('act', lambda e: e.activation(out=lbt[:, 0:4], in_=lbt[:, 8:12], func=AF.Sigmoid), ['lbt'], ['lbt'])
        S.op('act', lambda e: e.activation(out=lbt[:, 4:8], in_=lbt[:, 0:4], func=AF.Identity, scale=-1.0, bias=1.0), ['lbt'], ['lbt'])
        S.op('dve', lambda e: e.tensor_scalar(out=g05[:, 0:8], in0=vec[:, 8:16], scalar1=0.5, scalar2=None, op0=ALU.mult), ['vec'], ['g05'])
        S.op('dve', lambda e: e.tensor_scalar(out=g05[:, 8:16], in0=vec[:, 40:48], scalar1=0.5, scalar2=None, op0=ALU.mult), ['vec'], ['g05'])

        rstate = {'n': 0}

        def wload(src, L):
            s = rstate['n'] % NSLOT
            rstate['n'] += 1
            h = L // 2
            dma('pool', ring[:, s, 0:L].rearrange("p (a b) -> p a b", a=2), src.rearrange("p (a b) -> p a b", a=2),
                [], [('ring', s)], ('ring', s))
            return s

        def rview(s, kc, ncol):
            return ring[:, s, 0:kc * ncol].rearrange("p (k n) -> p k n", k=kc)

        def mm(out, lhsT, rhs, start, stop, reads, bank, wkeys=None):
            S.op('pe', lambda e: e.matmul(out, lhsT=lhsT, rhs=rhs, start=start, stop=stop), reads, [pk(bank)] if wkeys is None else wkeys)

        def rms_stat(srcs, keys, Dn, W, bank=6):
            n = len(srcs)
            for c, (a, k) in enumerate(zip(srcs, keys)):
                s2 = c % 2
                S.op('act', lambda e, a=a, s2=s2: e.activation(out=sq[:a.shape[0], s2, :W], in_=a, func=AF.Square), k, [('sq', s2)])
                P = a.shape[0]
                mm(PS[bank][:, :W], onesb[:P, :], sq[:P, s2, :W], c == 0, c == n - 1, [('sq', s2), 'onesb'], bank)
            S.op('act', lambda e: e.activation(out=rstd[:, :W], in_=PS[bank][:, :W], func=AF.Ln, scale=1.0 / Dn, bias=epst[:, 0:1]),
                 [pk(bank), 'epst'], ['rstd'])
            S.op('act', lambda e: e.activation(out=rstd[:, :W], in_=rstd[:, :W], func=AF.Exp, scale=-0.5), ['rstd'], ['rstd'])

        def stt(out, in0, scalar, in1, reads, writes, op0=ALU.mult, op1=ALU.mult):
            S.op('dve', lambda e: e.scalar_tensor_tensor(out=out, in0=in0, scalar=scalar, in1=in1, op0=op0, op1=op1), reads, writes)

        def tt(out, in0, in1, op, reads, writes, eng='dve'):
            S.op(eng, lambda e: e.tensor_tensor(out=out, in0=in0, in1=in1, op=op), reads, writes)

        def act(out, in_, func, reads, writes, **kw):
            S.op('act', lambda e: e.activation(out=out, in_=in_, func=func, **kw), reads, writes)

        def norm_to_xn(gn, W):
            rms_stat([hT[:, c, :W] for c in range(8)], [[('hT', c)] for c in range(8)], D, W)
            for c in range(8):
                stt(xn[:, c, :W], hT[:, c, :W], NG(gn, c), rstd[:, :W], [('hT', c), 'rstd', 'vec'], [('xn', c)])

        def rstd_from_bank(Dn, W, bank=6):
            S.op('act', lambda e: e.activation(out=rstd[:, :W], in_=PS[bank][:, :W], func=AF.Ln, scale=1.0 / Dn, bias=epst[:, 0:1]),
                 [pk(bank), 'epst'], ['rstd'])
            S.op('act', lambda e: e.activation(out=rstd[:, :W], in_=rstd[:, :W], func=AF.Exp, scale=-0.5), ['rstd'], ['rstd'])

        def residual_from_y(gap, W, stats_done=False):
            if stats_done:
                rstd_from_bank(D, W)
            else:
                rms_stat([yb[:, c, :W] for c in range(8)], [[('y', c)] for c in range(8)], D, W)
            for c in range(8):
                stt(yb[:, c, :W], yb[:, c, :W], gap(c), rstd[:, :W], [('y', c), 'rstd', 'vec', 'g05'], [('y', c)])
                tt(hT[:, c, :W], hT[:, c, :W], yb[:, c, :W], ALU.add, [('hT', c), ('y', c)], [('hT', c)])

        hmv = scr[:].rearrange("p a b -> p (a b)").bitcast(BF16)

        def hm(fc, W):
            i, j = fc // 2, fc % 2
            return hmv[:, i * 2 * SW + j * WMAX: i * 2 * SW + j * WMAX + W]

        def ffn(fi, W, gpre, gpost):
            norm_to_xn(gpre, W)
            for g in range(11):
                s = wload(WGU[fi, :, g, :], 4096)
                rv = ring[:, s, :].rearrange("p (m k n) -> p m k n", m=2, k=8)
                for j in range(2):
                    fc = 2 * g + j
                    b = fc % 2
                    for kc in range(8):
                        mm(PS[b][:, :W], rv[:, 0, kc, j * 128:(j + 1) * 128], xn[:, kc, :W], kc == 0, kc == 7, [('ring', s), ('xn', kc)], b)
                    for kc in range(8):
                        mm(PS[2 + b][:, :W], rv[:, 1, kc, j * 128:(j + 1) * 128], xn[:, kc, :W], kc == 0, kc == 7, [('ring', s), ('xn', kc)], 2 + b)
                    act(sg[:, b, :W], PS[b][:, :W], AF.Silu, [pk(b)], [('sg', b)])
                    tt(hm(fc, W), sg[:, b, :W], PS[2 + b][:, :W], ALU.mult, [('sg', b), pk(2 + b)], [('scr', fc // 2)])
            for dc in range(8):
                s = wload(WD[fi, :, dc, :], 2816)
                rv = rview(s, 22, 128)
                b = 4 + dc % 2
                for fc in range(22):
                    mm(PS[b][:, :W], rv[:, fc, :], hm(fc, W), fc == 0, fc == 21, [('ring', s), ('scr', fc // 2)], b)
                act(yb[:, dc, :W], PS[b][:, :W], AF.Copy, [pk(b)], [('y', dc)])
                act(sq[:, dc % 2, :W], PS[b][:, :W], AF.Square, [pk(b)], [('sq', dc % 2)])
                mm(PS[6][:, :W], onesb[:, :], sq[:, dc % 2, :W], dc == 0, dc == 7, [('sq', dc % 2), 'onesb'], 6)
            residual_from_y(gpost, W, stats_done=True)

        ocatf = ocat[:].rearrange("p c w -> p (c w)").bitcast(F32)
        qlatf = qlat[:].rearrange("p h c w -> p (h c w)").bitcast(F32)

        def xstage(blk):
            if blk < 2:
                return ocatf[:, blk * 1024:(blk + 1) * 1024], [('ocat', c) for c in range(blk * 4, blk * 4 + 4)]
            return qlatf[:, (blk - 2) * 1024:(blk - 1) * 1024], [('qlat', h) for h in range((blk - 2) * 2, (blk - 2) * 2 + 2)]

        def load_x_dma(XB, nxb, src):
            for blk in range(nxb):
                ap, keys = xstage(blk)
                dma('sp', ap[:XB, :], src[blk * XB:(blk + 1) * XB, :], [], keys, ('xs', blk))

        def load_x_T(W, XB, nxb):
            for dc in range(8):
                bk = dc % 4
                for blk in range(nxb):
                    ap, keys = xstage(blk)
                    S.op('pe', lambda e, dc=dc, blk=blk, ap=ap, bk=bk: e.transpose(out=PS[bk][:, blk * XB:(blk + 1) * XB], in_=ap[:XB, dc * 128:(dc + 1) * 128],
                                                                                   identity=identf[:XB, :XB]),
                         keys + ['identf'], [pk(bk)])
                if dc % 2 == 0:
                    act(hT[:, dc, :W], PS[bk][:, :W], AF.Copy, [pk(bk)], [('hT', dc)])
                else:
                    S.op('dve', lambda e, dc=dc, bk=bk: e.tensor_copy(out=hT[:, dc, :W], in_=PS[bk][:, :W]), [pk(bk)], [('hT', dc)])

        def store_y(W, XB, nxb, dst):
            ystage = yb[:].rearrange("p c w -> p (c w)").rearrange("p (b d) -> p b d", b=4)
            for blk in range(nxb):
                for half in range(2):
                    b = 4 + (blk * 2 + half) % 4
                    for j in range(4):
                        dc = half * 4 + j
                        S.op('pe', lambda e, dc=dc, blk=blk, b=b, j=j: e.transpose(out=PS[b][:XB, j * 128:(j + 1) * 128], in_=hT[:, dc, blk * XB:(blk + 1) * XB],
                                                                                  identity=identf[:, :]),
                             [('hT', dc), 'identf'], [pk(b)])
                    if half == 0:
                        act(ystage[:XB, blk, half * 512:(half + 1) * 512], PS[b][:XB, :], AF.Copy, [pk(b)], [('y', 2 * blk + half)])
                    else:
                        S.op('dve', lambda e, blk=blk, half=half, b=b: e.tensor_copy(out=ystage[:XB, blk, half * 512:(half + 1) * 512], in_=PS[b][:XB, :]),
                             [pk(b)], [('y', 2 * blk + half)])
                dma('sp', dst[blk * XB:(blk + 1) * XB, :], ystage[:XB, blk, :], [('y', 2 * blk), ('y', 2 * blk + 1)], [], ('ys', blk))

        outtoks = []
        marks = []
        def mark(name):
            marks.append((name, len(S.ops['pe'])))

        def mixer(ti, W, C, nch, TB, tok0, sample):
            norm_to_xn(2, W)
            ctok0 = tok0 if not sample else SEQ
            dma('sp', ropec[:, :W], RC[:, ctok0:ctok0 + W], [], ['ropec'], 'ropec')
            dma('sp', ropes[:, :W], RS[:, ctok0:ctok0 + W], [], ['ropes'], 'ropes')
            XN = [('xn', k) for k in range(8)]
            def v_piece():
                s = wload(WIN[:, 0, :], 4096)
                rv = rview(s, 8, 512)

                def vmm(c, bank):
                    for kc in range(8):
                        mm(PS[bank][:C, :], xn[:, kc, c * C:(c + 1) * C], rv[:, kc, :], kc == 0, kc == 7, [('ring', s), ('xn', kc)], bank)
                nfirst = min(4, nch)
                for c in range(nfirst):
                    vmm(c, 4 + c)

                def post():
                    for c in range(nfirst):
                        act(vtok[:C, c, :], PS[4 + c][:C, :], AF.Copy, [pk(4 + c)], [('vtok', c)])
                    for c in range(nfirst, nch):
                        bank = 4 + c % 4
                        vmm(c, bank)
                        act(vtok[:C, c, :], PS[bank][:C, :], AF.Copy, [pk(bank)], [('vtok', c)])
                return post

            def mla_cq(a):
                s = wload(WIN[:, 5 + a, 0:3072], 3072)
                rv = rview(s, 8, 384)
                for j in range(3):
                    b = 4 + j
                    for kc in range(8):
                        mm(PS[b][:, :W], rv[:, kc, j * 128:(j + 1) * 128], xn[:, kc, :W], kc == 0, kc == 7, [('ring', s), ('xn', kc)], b)

                def post():
                    for j in range(3):
                        oc = a * 3 + j
                        act(yb[:, oc, :W], PS[4 + j][:, :W], AF.Copy, [pk(4 + j)], [('y', oc)])
                return post

            def mla_kv():
                s = wload(WIN[:, 7, 0:3072], 3072)
                rv = rview(s, 8, 384)
                for rc in range(2):
                    for kc in range(8):
                        mm(PS[4 + rc][:, :W], rv[:, kc, rc * 128:(rc + 1) * 128], xn[:, kc, :W], kc == 0, kc == 7, [('ring', s), ('xn', kc)], 4 + rc)
                for j in range(2):
                    for kc in range(8):
                        mm(PS[6 + j][:64, :W], rv[:, kc, 256 + j * 64:256 + (j + 1) * 64], xn[:, kc, :W], kc == 0, kc == 7, [('ring', s), ('xn', kc)], 6 + j)
                return mla_kv_post

            def mla_kv_post():
                for rc in range(2):
                    act(yb[:, 6 + rc, :W], PS[4 + rc][:, :W], AF.Copy, [pk(4 + rc)], [('y', 6 + rc)])
                tt(krf[:, :W], PS[6][:64, :W], ropec[:, :W], ALU.mult, [pk(6), 'ropec'], ['krf'])
                tt(sg[:64, 0, :W], PS[7][:64, :W], ropes[:, :W], ALU.mult, [pk(7), 'ropes'], [('sg', 0)])
                tt(krf[:, :W], krf[:, :W], sg[:64, 0, :W], ALU.add, ['krf', ('sg', 0)], ['krf'])
                if not sample:
                    act(krT[:64, tok0:tok0 + W], krf[:, :W], AF.Copy, ['krf'], [('krT', ti)])
                rms_stat([yb[:, 6 + c, :W] for c in range(2)], [[('y', 6 + c)] for c in range(2)], 256.0, W)
                for rc in range(2):
                    stt(yb[:, 6 + rc, :W], yb[:, 6 + rc, :W], KG(rc), rstd[:, :W], [('y', 6 + rc), 'rstd', 'vec'], [('y', 6 + rc)])
                    if not sample:
                        act(ckvT[:, rc, tok0:tok0 + W], yb[:, 6 + rc, :W], AF.Copy, [('y', 6 + rc)], [('ckvT', ti)])
                for blk in range(4):
                    for rc in range(2):
                        S.op('pe', lambda e, blk=blk, rc=rc: e.transpose(out=PS[4][:TB, rc * 128:(rc + 1) * 128], in_=yb[:, 6 + rc, blk * TB:(blk + 1) * TB], identity=identf[:, :]),
                             [('y', 6 + rc), 'identf'], [pk(4)])
                    S.op('pe', lambda e, blk=blk: e.transpose(out=PS[4][:TB, 256:320], in_=krf[:, blk * TB:(blk + 1) * TB], identity=identf[:64, :64]),
                         ['krf', 'identf'], [pk(4)])
                    act(kvst[:TB, blk, :], PS[4][:TB, 0:256], AF.Copy, [pk(4)], [('kvst', blk)])
                    gblk = ti * 4 + blk if not sample else blk
                    if not sample:
                        act(ckvtok[:TB, gblk, :], PS[4][:TB, 0:256], AF.Copy, [pk(4)], [('ckvtok', gblk)])
                    else:
                        act(ckvtoks[:TB, blk, :], PS[4][:TB, 0:256], AF.Copy, [pk(4)], [('ckvtoks', blk)])
                    act(krst[:TB, blk, :], PS[4][:TB, 256:320], AF.Copy, [pk(4)], [('krst', blk)])
                for blk in range(4):
                    if not sample:
                        r0 = tok0 + blk * 128
                        outtoks.append(dma('sp', KVP[r0:r0 + 128, :], kvst[:, blk, :], [('kvst', blk)], [], ('kvo', blk)))
                        outtoks.append(dma('sp', KRP[r0:r0 + 128, :], krst[:, blk, :], [('krst', blk)], [], ('kro', blk)))
                    else:
                        outtoks.append(dma('sp', KVS[blk * 4:blk * 4 + 4, :], kvst[:4, blk, :], [('kvst', blk)], [], ('kvo', blk)))
                        outtoks.append(dma('sp', KRS[blk * 4:blk * 4 + 4, :], krst[:4, blk, :], [('krst', blk)], [], ('kro', blk)))

            mla_pieces = [lambda: mla_cq(0), lambda: mla_cq(1), mla_kv, v_piece]
            T = lambda i: scr[:, i, :W]
            TK = lambda i: [('scr', i)]
            qppH = [(qn if h < 2 else on)[:, h % 2, :W] for h in range(4)]
            qppK = [[('qn' if h < 2 else 'on', h % 2)] for h in range(4)]
            ATH = [(pT[:C, h, :nch * C] if h < 3 else sq[:C, 1, :nch * C]) for h in range(4)]
            ATK = [[('pT', h)] if h < 3 else [('sq', 1)] for h in range(4)]
            ktokH = [qlat[:C, h].rearrange("p a b -> p (a b)")[:, :nch * 128].rearrange("p (c k) -> p c k", k=128) for h in range(4)]
            ktokK = [[('qlat', h)] for h in range(4)]
            gateH = [ocat[:, 4 + h, :W] for h in range(4)]
            gateK = [[('ocat', 4 + h)] for h in range(4)]
            for h in range(4):
                s = wload(WIN[:, 1 + h, 0:3072], 3072)
                rv = rview(s, 8, 384)
                for j, bnk in enumerate((0, 1, 2)):
                    for kc in range(8):
                        mm(PS[bnk][:, :W], rv[:, kc, j * 128:(j + 1) * 128], xn[:, kc, :W], kc == 0, kc == 7, [('ring', s), ('xn', kc)], bnk)
                piece_post = mla_pieces[h]()
                act(T(0), PS[0][:, :W], AF.Silu, [pk(0)], TK(0))
                act(T(1), PS[1][:, :W], AF.Sigmoid, [pk(1)], TK(1))
                act(gateH[h], PS[2][:, :W], AF.Silu, [pk(2)], gateK[h])
                S.op('dve', lambda e, h=h: e.tensor_scalar(out=T(1), in0=T(1), scalar1=lbt[:, 4 + h:5 + h], scalar2=lbt[:, h:h + 1], op0=ALU.mult, op1=ALU.add),
                     TK(1) + ['lbt'], TK(1))
                act(T(2), T(1), AF.Ln, TK(1), TK(2))
                act(T(3), T(1), AF.Identity, TK(1), TK(3), scale=-1.0, bias=1.0)
                S.op('dve', lambda e: e.memset(scr[:, 5, 0:1], 0.0), [], TK(5))
                S.op('dve', lambda e: e.tensor_tensor_scan(out=scr[:, 5, 1:W + 1], data0=bc(onesf[:, 0:1], [[0, W]]), data1=T(2), initial=0.0, op0=ALU.mult, op1=ALU.add),
                     TK(2) + ['onesf'], TK(5))
                Bv1 = bc(scr[:, 5, 1:2], [[C, nch], [1, C]])
                Bmid = bc(scr[:, 5, C // 2:C // 2 + 1], [[C, nch], [0, C]])
                Bst = bc(scr[:, 5, 0:1], [[C, nch], [0, C]])
                Bla = bc(scr[:, 5, C:C + 1], [[C, nch], [0, C]])
                T6 = bc(scr[:, 6, 0:1], [[C, nch], [1, C]])
                qp = hmv[:, 8 * 2 * SW: 8 * 2 * SW + W]
                kp = hmv[:, 8 * 2 * SW + WMAX: 8 * 2 * SW + WMAX + W]
                kppp = hmv[:, 9 * 2 * SW + WMAX: 9 * 2 * SW + WMAX + W]
                tt(T6, Bv1, Bmid, ALU.subtract, TK(5), TK(6))
                act(T(7), T(6), AF.Exp, TK(6), TK(7))
                tt(qp, T(0), T(7), ALU.mult, TK(0) + TK(7), TK(8))
                act(T(7), T(6), AF.Exp, TK(6), TK(7), scale=-1.0)
                tt(kp, T(3), T(7), ALU.mult, TK(3) + TK(7), TK(8))
                Bm1 = bc(scr[:, 5, C // 2:C // 2 + 1], [[C, nch]])
                Bs1 = bc(scr[:, 5, 0:1], [[C, nch]])
                Bl1 = bc(scr[:, 5, C:C + 1], [[C, nch]])
                tt(cst4[:, h, 0, :nch], Bm1, Bs1, ALU.subtract, TK(5), [('dec', h)])
                tt(cst4[:, h, 1, :nch], Bl1, Bm1, ALU.subtract, TK(5), [('dec', h)])
                tt(cst4[:, h, 2, :nch], Bl1, Bs1, ALU.subtract, TK(5), [('dec', h)])
                act(cst4[:, h, :, :nch], cst4[:, h, :, :nch], AF.Exp, [('dec', h)], [('dec', h)])
                tt(qppH[h].rearrange("p (c j) -> p c j", j=C), qp.rearrange("p (c j) -> p c j", j=C),
                   bc(cst4[:, h, 0, 0:1], [[1, nch], [0, C]]), ALU.mult, TK(8) + [('dec', h)], qppK[h])
                tt(kppp.rearrange("p (c j) -> p c j", j=C), kp.rearrange("p (c j) -> p c j", j=C),
                   bc(cst4[:, h, 1, 0:1], [[1, nch], [0, C]]), ALU.mult, TK(8) + [('dec', h)], TK(9))
                for c in range(nch):
                    S.op('pe', lambda e, c=c, kppp=kppp: e.transpose(out=PSB[3][:C, c * 128:(c + 1) * 128], in_=kppp[:, c * C:(c + 1) * C], identity=identb[:, :]),
                         TK(9) + ['identb'], [pk(3)])
                S.op('act', lambda e, h=h: e.activation(out=ktokH[h], in_=PSB[3][:C, :nch * 128].rearrange("p (c k) -> p c k", k=128), func=AF.Copy),
                     [pk(3)], ktokK[h])
                for c in range(nch):
                    mm(PS[0][:C, c * C:(c + 1) * C], kp[:, c * C:(c + 1) * C], qp[:, c * C:(c + 1) * C], True, True, TK(8), 0)
                tt(ATH[h].rearrange("p (c j) -> p c j", j=C), PS[0][:C, :nch * C].rearrange("p (c j) -> p c j", j=C),
                   bc(maskh[:C, 0:1], [[0, nch], [1, C]]), ALU.mult, [pk(0), 'maskh'], ATK[h])
                piece_post()
            for c in range(nch):
                for h in range(4):
                    sid = h if not sample else 4 + c * 4 + h
                    cs = slice(c * C, (c + 1) * C)
                    mm(PS[4 + h][:, cs], vtok[:C, c, h * 128:(h + 1) * 128], ATH[h][:, cs], True, False, [('vtok', c)] + ATK[h], 4 + h)
                    mm(PS[4 + h][:, cs], Sb[:, sid, :], qppH[h][:, cs], False, True, [('Sb', sid)] + qppK[h], 4 + h)
                    mm(PS[h][:, :128], ktokH[h][:, c, :], vtok[:C, c, h * 128:(h + 1) * 128], True, True, ktokK[h] + [('vtok', c)], h)
                    stt(Sf[:, sid, :], Sf[:, sid, :], cst4[:, h, 2, c:c + 1], PS[h][:, :128], [('Sf', sid), ('dec', h), pk(h)], [('Sf', sid)], op0=ALU.mult, op1=ALU.add)
                    if not sample:
                        act(Sb[:, sid, :], Sf[:, sid, :], AF.Copy, [('Sf', sid)], [('Sb', sid)])
            for h in range(4):
                rms_stat([PS[4 + h][:, :W]], [[pk(4 + h)]], 128.0, W, bank=h)
                stt(T(6), PS[4 + h][:, :W], HG, rstd[:, :W], [pk(4 + h), 'rstd', 'vec'], TK(6))
                tt(ocat[:, h, :W], T(6), gateH[h], ALU.mult, TK(6) + gateK[h], [('ocat', h)])
            mark('hgrn_end')
            if stages < 3:
                return
            rms_stat([yb[:, c, :W] for c in range(6)], [[('y', c)] for c in range(6)], 768.0, W)
            for c in range(6):
                stt(xn[:, c, :W], yb[:, c, :W], QG(c), rstd[:, :W], [('y', c), 'rstd', 'vec'] + XN, [('xn', c)])
            if stages < 3.7:
                return
            sA = wload(WQ[:, 0, :], 3072)
            rvA = rview(sA, 6, 512)
            wuk = wukv[:, 0:1024].rearrange("p (h r) -> p h r", h=4)
            wuv = wukv[:, 1024:2048].rearrange("p (h c v) -> p h c v", h=4, c=2)
            for h in range(4):
                b = h % 2
                for kc in range(6):
                    mm(PS[b][:, :W], rvA[:, kc, h * 128:(h + 1) * 128], xn[:, kc, :W], kc == 0, kc == 5, [('ring', sA), ('xn', kc)], b)
                act(qn[:, b, :W], PS[b][:, :W], AF.Copy, [pk(b)], [('qn', b)])
                for rc in range(2):
                    mm(PS[2 + rc][:, :W], wuk[:, h, rc * 128:(rc + 1) * 128], qn[:, b, :W], True, True, [('qn', b), 'wukv'], 2 + rc)
                    act(qlat[:, h, rc, :W], PS[2 + rc][:, :W], AF.Copy, [pk(2 + rc)], [('qlat', h)])
            sB = wload(WQ[:, 1, :], 3072)
            rvB = rview(sB, 6, 512)
            for h in range(4):
                for j in range(2):
                    for kc in range(6):
                        mm(PS[4 + j][:64, :W], rvB[:, kc, j * 256 + h * 64: j * 256 + (h + 1) * 64], xn[:, kc, :W], kc == 0, kc == 5, [('ring', sB), ('xn', kc)], 4 + j)
                tt(scr[:64, 6, :W], PS[4][:64, :W], ropec[:, :W], ALU.mult, [pk(4), 'ropec'], TK(6))
                tt(scr[:64, 7, :W], PS[5][:64, :W], ropes[:, :W], ALU.mult, [pk(5), 'ropes'], TK(7))
                tt(qr[:64, h, :W], scr[:64, 6, :W], scr[:64, 7, :W], ALU.add, TK(6) + TK(7), [('qr', h)])
            mark('proj_end')
            if stages < 4:
                return
            if not sample:
                nkb = 4 * (ti + 1)
                steps = [(h, j) for h in range(4) for j in range(nkb)]

                def scores(n):
                    h, j = steps[n]
                    b = n % 2
                    ks = slice(j * 128, (j + 1) * 128)
                    kti = j // 4
                    c0 = max(0, j - 4 * ti) * 128
                    cs = slice(c0, W)
                    mm(PS[b][:, cs], ckvT[:, 0, ks], qlat[:, h, 0, cs], True, False, [('ckvT', kti), ('qlat', h)], b)
                    mm(PS[b][:, cs], ckvT[:, 1, ks], qlat[:, h, 1, cs], False, False, [('ckvT', kti), ('qlat', h)], b)
                    mm(PS[b][:, cs], krT[:, ks], qr[:, h, cs], False, True, [('krT', kti), ('qr', h)], b)
                    pb = n % 3
                    act(pT[:, pb, cs], PS[b][:, cs], AF.Exp, [pk(b)], [('pT', pb)], scale=SCALE)
                    if j >= 4 * ti:
                        dsl = slice(c0, c0 + 128)
                        tt(pT[:, pb, dsl], pT[:, pb, dsl], maskp[:, 0:128], ALU.mult, [('pT', pb), 'maskp'], [('pT', pb)])

                def pv(n):
                    h, j = steps[n]
                    pb = n % 3
                    c0 = max(0, j - 4 * ti) * 128
                    cs = slice(c0, W)
                    for rc in range(2):
                        mm(PS[2 + rc][:, cs], ckvtok[:, j, rc * 128:(rc + 1) * 128], pT[:, pb, cs], j == 0, j == nkb - 1, [('ckvtok', j), ('pT', pb)], 2 + rc)
                    mm(PS[4][:, cs], onesb[:, :], pT[:, pb, cs], j == 0, j == nkb - 1, ['onesb', ('pT', pb)], 4)
                    if j == nkb - 1:
                        act(rstd[:, :W], PS[4][:, :W], AF.Ln, [pk(4)], ['rstd'])
                        act(rstd[:, :W], rstd[:, :W], AF.Exp, ['rstd'], ['rstd'], scale=-1.0)
                        for rc in range(2):
                            tt(on[:, rc, :W], PS[2 + rc][:, :W], rstd[:, :W], ALU.mult, [pk(2 + rc), 'rstd'], [('on', rc)])
                        for rc in range(2):
                            mm(PS[5][:, :W], wuv[:, h, rc, :], on[:, rc, :W], rc == 0, rc == 1, [('on', rc), 'wukv'], 5)
                        act(ocat[:, 4 + h, :W], PS[5][:, :W], AF.Copy, [pk(5)], [('ocat', 4 + h)])

                scores(0)
                for n in range(len(steps)):
                    if n + 1 < len(steps):
                        scores(n + 1)
                    pv(n)
            else:
                sample_attention(W, wuv)

        def scrk(lo, hi):
            return [('scr', i) for i in range(lo // (2 * SW), (hi - 1) // (2 * SW) + 1)]

        def sample_attention(W, wuv):
            ybf = yb[:].rearrange("p c w -> p (c w)").bitcast(BF16)
            xnf = xn[:].rearrange("p c w -> p (c w)")
            for rc in range(2):
                act(on[:, rc, :W], yb[:, 6 + rc, :W], AF.Copy, [('y', 6 + rc)], [('on', rc)])
            act(pT[:64, 2, :W], krf[:, :W], AF.Copy, ['krf'], [('pT', 2)])
            GR0 = 8192
            PN0 = 11264
            ONB0 = 11328
            seq = [(b, tg) for b in range(4) for tg in range(8)]

            ckvT_flat = ckvT[:].rearrange("p c s -> p (c s)")
            ckvtok_flat = ckvtok[:].rearrange("p b d -> p (b d)")

            def bufs(n):
                b, tg = seq[n]
                g3 = n % 4
                gb = n % 2
                if g3 < 2:
                    kgk = scrk(g3 * 4096, (g3 + 1) * 4096)
                    gk = hmv[:, g3 * 4096:(g3 + 1) * 4096]
                elif g3 == 2:
                    kgk = [('ckvT', i) for i in range(4)]
                    gk = ckvT_flat
                else:
                    kgk = [('ckvtok', i) for i in range(16)]
                    gk = ckvtok_flat
                if g3 < 3:
                    kgr = scrk(GR0 + g3 * 1024, GR0 + (g3 + 1) * 1024)
                    gr = hmv[:, GR0 + g3 * 1024: GR0 + (g3 + 1) * 1024]
                else:
                    kgr = ['maskp_hi']
                    gr = maskp[:, 1024:2048]
                kkT = [('y', c) for c in range(gb * 4, gb * 4 + 4)]
                krT_ = [('xn', c) for c in range(gb * 4, gb * 4 + 4)]
                gk3 = gk.rearrange("p (t d) -> p t d", d=256)
                gr3 = gr.rearrange("p (t d) -> p t d", d=64)
                kTb = ybf[:, gb * 4096:(gb + 1) * 4096].rearrange("p (t c k) -> p t c k", t=16, c=2)
                rTb = xnf[:64, gb * 2048:(gb + 1) * 2048].rearrange("p (t k) -> p t k", t=16)
                return b, tg, g3, kgk, kgr, kkT, krT_, gk, gr, gk3, gr3, kTb, rTb

            def Gstage(n):
                b, tg, g3, kgk, kgr, kkT, krT_, gk, gr, gk3, gr3, kTb, rTb = bufs(n)
                S.op('dve', lambda e: e.tensor_scalar(out=idx[:, g3:g3 + 1], in0=pti[:, b:b + 1], scalar1=8, scalar2=tg, op0=ALU.mult, op1=ALU.add),
                     ['pti'], [('idx', g3)])
                S.op('pool', lambda e: e.indirect_dma_start(out=gk, out_offset=None, in_=CKV, in_offset=bass.IndirectOffsetOnAxis(ap=idx[:, g3:g3 + 1], axis=0)),
                     [('idx', g3)], kgk, dma=('gk', g3))
                S.op('pool', lambda e: e.indirect_dma_start(out=gr, out_offset=None, in_=CKR, in_offset=bass.IndirectOffsetOnAxis(ap=idx[:, g3:g3 + 1], axis=0)),
                     [('idx', g3)], kgr, dma=('gr', g3))

            def Tstage(n):
                b, tg, g3, kgk, kgr, kkT, krT_, gk, gr, gk3, gr3, kTb, rTb = bufs(n)
                for q4 in range(4):
                    tb = q4 % 2
                    for t4 in range(4):
                        t = q4 * 4 + t4
                        for rc in range(2):
                            S.op('pe', lambda e, t=t, rc=rc, t4=t4, tb=tb: e.transpose(out=PSB[tb][:, (t4 * 2 + rc) * 128:(t4 * 2 + rc + 1) * 128],
                                                                               in_=gk3[:, t, rc * 128:(rc + 1) * 128], identity=identb[:, :]),
                                 kgk + ['identb'], [pk(tb)])
                    src4 = PSB[tb][:, :].rearrange("p (t c k) -> p t c k", t=4, c=2)
                    if q4 % 2 == 0:
                        S.op('act', lambda e, q4=q4, src4=src4: e.activation(out=kTb[:, q4 * 4:(q4 + 1) * 4, :, :], in_=src4, func=AF.Copy), [pk(tb)], kkT)
                    else:
                        S.op('dve', lambda e, q4=q4, src4=src4: e.tensor_copy(out=kTb[:, q4 * 4:(q4 + 1) * 4, :, :], in_=src4), [pk(tb)], kkT)
                for q8 in range(2):
                    rbk = 7 if q8 == 0 else 3
                    for t8 in range(8):
                        t = q8 * 8 + t8
                        S.op('pe', lambda e, t=t, t8=t8, rbk=rbk: e.transpose(out=PSB[rbk][:64, t8 * 128:(t8 + 1) * 128], in_=gr3[:, t, :], identity=identb[:, :]),
                             kgr + ['identb'], [pk(rbk)])
                    src8 = PSB[rbk][:64, :].rearrange("p (t k) -> p t k", t=8)
                    if q8 == 0:
                        S.op('dve', lambda e, q8=q8, src8=src8: e.tensor_copy(out=rTb[:, q8 * 8:(q8 + 1) * 8, :], in_=src8), [pk(rbk)], krT_)
                    else:
                        S.op('act', lambda e, q8=q8, src8=src8: e.activation(out=rTb[:, q8 * 8:(q8 + 1) * 8, :], in_=src8, func=AF.Copy), [pk(rbk)], krT_)

            def Sstage(n):
                b, tg, gb, kgk, kgr, kkT, krT_, gk, gr, gk3, gr3, kTb, rTb = bufs(n)
                qs = slice(4 * b, 4 * b + 4)
                half = n % 2
                gb2 = n % 2
                wk = [('ps2h', half)] + ([pk(2)] if n < 2 else [])
                for t in range(16):
                    o3 = PS[2][:, half * 256 + t * 16: half * 256 + (t + 1) * 16].rearrange("p (h t) -> p h t", h=4)
                    mm(o3, kTb[:, t, 0, :], qlat[:, :, 0, qs], True, False, kkT + [('qlat', h) for h in range(4)], 2, wk)
                    mm(o3, kTb[:, t, 1, :], qlat[:, :, 1, qs], False, False, kkT, 2, wk)
                    mm(o3, xnf[:, gb2 * 2048:(gb2 + 1) * 2048].rearrange("p (t k) -> p t k", t=16)[:, t, :], qr[:, :, qs], False, True, krT_ + [('qr', h) for h in range(4)], 2, wk)
                pb = n % 2
                act(pT[:, pb, 0:256], PS[2][:, half * 256: half * 256 + 256], AF.Exp, [('ps2h', half)] + ([pk(2)] if n >= len(seq) - 2 else []), [('pT', pb)], scale=SCALE)

            def Vstage(n):
                b, tg, gb, kgk, kgr, kkT, krT_, gk, gr, gk3, gr3, kTb, rTb = bufs(n)
                qs = slice(4 * b, 4 * b + 4)
                pb = n % 2
                for t in range(16):
                    pc = pT[:, pb, t * 16:(t + 1) * 16]
                    st0_ = (tg == 0 and t == 0)
                    for rc in range(2):
                        mm(PS[4 + rc][:, 0:16], gk3[:, t, rc * 128:(rc + 1) * 128], pc, st0_, False, kgk + [('pT', pb)], 4 + rc)
                    mm(PS[6][:, 0:16], onesb[:, :], pc, st0_, False, ['onesb', ('pT', pb)], 6)
                if tg != 7:
                    return
                o3 = PS[7][:4, 0:16].rearrange("p (h t) -> p h t", h=4)
                mm(o3, on[:, 0, qs], qlat[:, :, 0, qs], True, False, [('on', 0)], 7)
                mm(o3, on[:, 1, qs], qlat[:, :, 1, qs], False, False, [('on', 1)], 7)
                mm(o3, pT[:, 2, qs], qr[:, :, qs], False, True, [('pT', 2)], 7)
                act(sg[:4, 0, 0:16], PS[7][:4, 0:16], AF.Exp, [pk(7)], [('sg', 0)], scale=SCALE)
                kpn = scrk(PN0, PN0 + 16)
                pn = hmv[:4, PN0:PN0 + 16]
                tt(pn.rearrange("p (h t) -> p h t", h=4), sg[:4, 0, 0:16].rearrange("p (h t) -> p h t", h=4), bc(maskh[:4, 0:1], [[0, 4], [1, 4]]), ALU.mult,
                   [('sg', 0), 'maskh'], kpn)
                for rc in range(2):
                    mm(PS[4 + rc][:, 0:16], ckvtoks[:4, b, rc * 128:(rc + 1) * 128], pn, False, True, [('ckvtoks', b)] + kpn, 4 + rc)
                mm(PS[6][:, 0:16], onesb[:4, :], pn, False, True, ['onesb'] + kpn, 6)
                act(rstd[:, 0:16], PS[6][:, 0:16], AF.Ln, [pk(6)], ['rstd'])
                act(rstd[:, 0:16], rstd[:, 0:16], AF.Exp, ['rstd'], ['rstd'], scale=-1.0)
                konb = scrk(ONB0, ONB0 + 32)
                onb = hmv[:, ONB0:ONB0 + 32].rearrange("p (c n) -> p c n", c=2)
                for rc in range(2):
                    tt(onb[:, rc, :], PS[4 + rc][:, 0:16], rstd[:, 0:16], ALU.mult, [pk(4 + rc), 'rstd'], konb)
                for h in range(4):
                    for rc in range(2):
                        mm(PS[7][:, 32 + h * 4: 32 + h * 4 + 4], wuv[:, h, rc, :], onb[:, rc, h * 4:(h + 1) * 4], rc == 0, rc == 1, konb + ['wukv'], 7)
                for h in range(4):
                    act(ocat[:, 4 + h, qs], PS[7][:, 32 + h * 4:32 + h * 4 + 4], AF.Copy, [pk(7)], [('ocat', 4 + h)])

            Gstage(0)
            Gstage(1)
            Gstage(2)
            Tstage(0)
            for n in range(len(seq)):
                Sstage(n)
                if n + 3 < len(seq):
                    Gstage(n + 3)
                if n + 1 < len(seq):
                    Tstage(n + 1)
                Vstage(n)

        def wout(W):
            for g in range(2):
                s = wload(WO[:, g, :], 4096)
                rv = rview(s, 8, 512)
                for j in range(4):
                    dc = g * 4 + j
                    b = dc % 2
                    for c in range(8):
                        mm(PS[b][:, :W], rv[:, c, j * 128:(j + 1) * 128], ocat[:, c, :W], c == 0, c == 7, [('ring', s), ('ocat', c)], b)
                    act(yb[:, dc, :W], PS[b][:, :W], AF.Copy, [pk(b)], [('y', dc)])
            residual_from_y(lambda c: NG(3, c), W)

        tiles = [(i, WMAX, 64, 8, 128, i * WMAX, False) for i in range(4)] + [(4, 16, 4, 4, 4, 0, True)]
        for (ti, W, C, nch, TB, tok0, sample) in tiles:
            if ti == 0:
                load_x_dma(128, 4, XP[0:WMAX, :])
            load_x_T(W, 128 if not sample else 16, 4 if not sample else 1)
            mark('tile%d_start' % ti)
            if stages >= 1:
                ffn(0, W, 0, lambda c: g05[:, c:c + 1])
            mark('ffn1_end')
            if stages >= 2:
                mixer(ti, W, C, nch, TB, tok0, sample)
            mark('attn_end')
            if stages >= 5:
                wout(W)
            mark('wout_end')
            if ti < 3:
                load_x_dma(128, 4, XP[(ti + 1) * WMAX:(ti + 2) * WMAX, :])
            elif ti == 3:
                load_x_dma(16, 1, XS)
            if stages >= 6:
                ffn(1, W, 4, lambda c: g05[:, 8 + c:9 + c])
            mark('ffn2_end')
            if not sample:
                store_y(W, 128, 4, YP[tok0:tok0 + W, :])
            else:
                store_y(W, 16, 1, YS)
            if not sample and ti == 3 and stages >= 2:
                outtoks.append(dma('sp', STP.rearrange("h k v -> k h v"), Sf[:, 0:4, :], [('Sf', i) for i in range(4)], [], 'stp'))
            if sample and stages >= 2:
                outtoks.append(dma('sp', STS.rearrange("s k v -> k s v"), Sf[:, 4:20, :], [('Sf', i) for i in range(4, 20)], [], 'sts'))

        toks = list(outtoks)
        for key, sem in S.dma_sems.items():
            toks.append((sem, S.dma_cum[key], 'dma', ('d', key)))
        S.wait_tokens('sp', toks)
        S.emit()
    nc._marks = marks
    return nc


def _consts():
    ident = np.eye(128, dtype=np.float32)
    s = np.arange(128)[:, None]
    q = np.arange(512)[None, :]
    maskp = np.concatenate([(o * 128 + s <= q).astype(np.float32) for o in range(4)], axis=1)
    maskh = (np.arange(64)[:, None] <= np.arange(64)[None, :]).astype(np.float32)
    half = 32
    inv = (np.float64(10000.0) ** (-np.arange(half, dtype=np.float64) / half)).astype(np.float32)
    pos = np.concatenate([np.arange(SEQ), np.tile(16384 + np.arange(4), 4)]).astype(np.float32)
    ang = (pos[None, :] * inv[:, None]).astype(np.float32).astype(np.float64)
    cos = np.cos(ang).astype(np.float32)
    sin = np.sin(ang).astype(np.float32)
    ropec = np.concatenate([cos, cos], axis=0)
    ropes = np.concatenate([-sin, sin], axis=0)
    return ident, maskp, maskh, np.ascontiguousarray(ropec), np.ascontiguousarray(ropes)


def _layout_weights(inp):
    f = lambda a: np.ascontiguousarray(a, dtype=np.float32)
    wg, wu, wd = inp["w_ffn_gate"][0], inp["w_ffn_up"][0], inp["w_ffn_down"][0]
    wgu = np.empty((2, 128, 11, 2, 8, 256), np.float32)
    wdl = np.empty((2, 128, 8, 22, 128), np.float32)
    for fi in range(2):
        a = wg[fi].reshape(8, 128, 11, 256).transpose(1, 2, 0, 3)
        b = wu[fi].reshape(8, 128, 11, 256).transpose(1, 2, 0, 3)
        wgu[fi, :, :, 0] = a
        wgu[fi, :, :, 1] = b
        wdl[fi] = wd[fi].reshape(22, 128, 8, 128).transpose(1, 2, 0, 3)
    wgu = wgu.reshape(2, 128, 11, 4096)
    wdl = wdl.reshape(2, 128, 8, 2816)
    win = inp["w_in"][0]
    def grp(cols):
        a = win[:, cols].reshape(8, 128, len(cols)).transpose(1, 0, 2)
        out = np.zeros((128, 4096), np.float32)
        out[:, :8 * len(cols)] = a.reshape(128, -1)
        return out
    ar = np.arange
    groups = [grp(ar(1024, 1536))]
    for h in range(4):
        groups.append(grp(np.concatenate([ar(h * 128, (h + 1) * 128), 512 + ar(h * 128, (h + 1) * 128), 1536 + ar(h * 128, (h + 1) * 128)])))
    groups.append(grp(ar(2048, 2048 + 384)))
    groups.append(grp(ar(2048 + 384, 2816)))
    sw = np.concatenate([ar(32, 64), ar(0, 32)])
    groups.append(grp(np.concatenate([ar(2816, 3072), 3072 + ar(64), 3072 + sw])))
    winl = np.stack(groups, axis=1)
    wq = inp["w_q_up"][0].reshape(768, 768)
    colsA = np.concatenate([h * 192 + ar(128) for h in range(4)])
    colsB = np.concatenate([h * 192 + 128 + ar(64) for h in range(4)] + [h * 192 + 128 + sw for h in range(4)])
    def grq(cols):
        return wq[:, cols].reshape(6, 128, 512).transpose(1, 0, 2).reshape(128, 3072)
    wql = np.stack([grq(colsA), grq(colsB)], axis=1)
    wkv = inp["w_kv_up"][0]
    wuk = wkv[:, :, :128].transpose(2, 1, 0).reshape(128, 1024)
    wuv = wkv[:, :, 128:].reshape(2, 128, 4, 128).transpose(1, 2, 0, 3).reshape(128, 1024)
    wukv = np.concatenate([wuk, wuv], axis=1)
    wo = inp["w_out"][0].reshape(8, 128, 2, 512).transpose(1, 2, 0, 3).reshape(128, 2, 4096)
    vec = np.zeros((128, 72), np.float32)
    vec[:, 0:48] = inp["norm_g"][0].reshape(6, 8, 128).transpose(2, 0, 1).reshape(128, 48)
    vec[:, 48:54] = inp["q_norm_g"][0].reshape(6, 128).T
    vec[:, 54:56] = inp["kv_norm_g"][0].reshape(2, 128).T
    vec[:, 56] = inp["hg_norm_g"][0]
    lbl = inp["hgrn_lb_logits"]
    vec[:, 57:61] = lbl[0].reshape(4, 128).T
    vec[:, 61:65] = lbl[1].reshape(4, 128).T
    return f(wgu), f(wdl), f(winl), f(wql), f(wukv), f(wo), vec


_CACHE = {}


def kernel(**inp):
    inp = {k: np.asarray(v) for k, v in inp.items()}
    ident, maskp, maskh, ropec, ropes = _consts()
    wgu, wdl, winl, wql, wukv, wo, vec2 = _layout_weights(inp)
    nc = _CACHE.get('nc')
    if nc is None:
        nc = build()
        _CACHE['nc'] = nc
    ckv = np.ascontiguousarray(inp["cache_kv_latent"][0]).reshape(5120 * 8, 16 * KVL)[:CACHE_ROWS]
    ckr = np.ascontiguousarray(inp["cache_k_rope"][0]).reshape(5120 * 8, 16 * ROPE)[:CACHE_ROWS]
    in_maps = []
    for c in range(NCORES):
        pt = np.ascontiguousarray(inp["page_table"][4 * c:4 * c + 4].T.astype(np.int32))
        st0 = np.ascontiguousarray(inp["state_hgrn"][0, 4 * c:4 * c + 4].reshape(16, 128, 128).transpose(1, 0, 2))
        in_maps.append({
            "xp": np.ascontiguousarray(inp["x_prompt"][c]), "xs": np.ascontiguousarray(inp["x_sample"][4 * c:4 * c + 4].reshape(16, D)),
            "cache_kv": ckv, "cache_kr": ckr, "st0": st0, "pt": pt, "vec": vec2, "wgu": wgu, "wd": wdl, "win": winl, "wq": wql,
            "wukv": wukv, "wo": wo, "ident": ident, "maskp": maskp, "maskh": maskh, "ropec": ropec, "ropes": ropes,
        })
    res = run_bass_kernel_spmd(nc, in_maps, core_ids=list(range(NCORES)))
    R = res.results
    y_p = np.stack([R[c]["yp"] for c in range(NCORES)])
    y_s = np.concatenate([R[c]["ys"].reshape(4, 4, D) for c in range(NCORES)])
    kv_p = np.stack([R[c]["kvp"] for c in range(NCORES)])[None]
    kr_p = np.stack([R[c]["krp"] for c in range(NCORES)])[None]
    st_p = np.stack([R[c]["stp"] for c in range(NCORES)])[None]
    kv_s = np.concatenate([R[c]["kvs"].reshape(4, 4, KVL) for c in range(NCORES)])[None]
    kr_s = np.concatenate([R[c]["krs"].reshape(4, 4, ROPE) for c in range(NCORES)])[None]
    st_s = np.concatenate([R[c]["sts"].reshape(4, 4, 128, 128) for c in range(NCORES)])[None]
    return (y_p, y_s, kv_p, kr_p, st_p, kv_s, kr_s, st_s)
```

```python
import math
from contextlib import ExitStack
import numpy as np
import concourse.bass as bass
import concourse.mybir as mybir
from concourse.bass_utils import run_bass_kernel_spmd

F32 = mybir.dt.float32
BF16 = mybir.dt.bfloat16
I32 = mybir.dt.int32
AF = mybir.ActivationFunctionType
ALU = mybir.AluOpType

NCORES = 8
D = 1024
DFF = 2816
SEQ = 2048
DEC_B = 32
DEC_T = 4
PAGE = 128
NPAGES = 128
KVL = 256
ROPE = 64
EPS = 1e-6
SCALE = 1.0 / math.sqrt(192.0)
SEM_CAP = 8000
SLOT = 4096
NSLOT = 5
WMAX = 512
CACHE_ROWS = 5120 * 8
SW = 528


class Sched:
    def __init__(self, nc, stack):
        self.nc = nc
        self.stack = stack
        self.engs = {'pe': nc.tensor, 'act': nc.scalar, 'dve': nc.vector, 'pool': nc.gpsimd, 'sp': nc.sync}
        self.ops = {e: [] for e in self.engs}
        self.cnt = {e: 0 for e in self.engs}
        self.cursem = {}
        self.nsem = 0
        for e in self.engs:
            self.cursem[e] = self._newsem()
        self.waited = {e: {} for e in self.engs}
        self.lastw = {}
        self.readers = {}
        self.dma_cum = {}
        self.dma_sems = {}

    def _newsem(self):
        self.nsem += 1
        return self.stack.enter_context(self.nc.semaphore(f"s{self.nsem}"))

    def op(self, eng, fn, reads=(), writes=(), dma=None):
        deps = {}

        def add(tok):
            if tok is None:
                return
            sem, val, src, sid = tok
            if eng == 'pe' and src == 'pe':
                return
            if self.waited[eng].get(sid, 0) >= val:
                return
            if sid not in deps or deps[sid][1] < val:
                deps[sid] = (sem, val)
        for r in reads:
            add(self.lastw.get(r))
        for w in writes:
            add(self.lastw.get(w))
            for t in self.readers.get(w, ()):
                add(t)
        waits = []
        for sid, (sem, val) in deps.items():
            self.waited[eng][sid] = val
            waits.append((sem, val))
        if dma is not None:
            if dma not in self.dma_sems:
                self.dma_sems[dma] = self._newsem()
                self.dma_cum[dma] = 0
            self.dma_cum[dma] += 16
            tok = (self.dma_sems[dma], self.dma_cum[dma], 'dma', ('d', dma))
            inc = 16
        else:
            if self.cnt[eng] >= SEM_CAP:
                self.cursem[eng] = self._newsem()
                self.cnt[eng] = 0
            self.cnt[eng] += 1
            tok = (self.cursem[eng], self.cnt[eng], eng, id(self.cursem[eng]))
            inc = 1
        for w in writes:
            self.lastw[w] = tok
            self.readers[w] = []
        for r in reads:
            if r not in writes:
                self.readers.setdefault(r, []).append(tok)
        self.ops[eng].append((waits, fn, tok[0], inc))
        return tok

    def wait_tokens(self, eng, toks):
        waits = []
        for tok in toks:
            sem, val, src, sid = tok
            if self.waited[eng].get(sid, 0) >= val:
                continue
            self.waited[eng][sid] = val
            waits.append((sem, val))
        self.ops[eng].append((waits, None, None, 0))

    def emit(self):
        nc = self.nc
        with nc.Block() as block:
            def run(name):
                def body(e):
                    for waits, fn, sem, inc in self.ops[name]:
                        for (s, v) in waits:
                            e.wait_ge(s, v)
                        if fn is not None:
                            fn(e).then_inc(sem, inc)
                return body
            block.sync(run('sp'))
            block.scalar(run('act'))
            block.vector(run('dve'))
            block.gpsimd(run('pool'))
            block.tensor(run('pe'))


def bc(ap, dims):
    return bass.AP(ap.tensor, ap.offset, [list(ap.ap[0])] + [list(d) for d in dims])


def build(stages=99, debug=False):
    nc = bass.Bass("TRN2", target_bir_lowering=False)
    di = lambda name, shape, dt=F32: nc.dram_tensor(name, list(shape), dt, kind="ExternalInput").ap()
    do = lambda name, shape, dt=F32: nc.dram_tensor(name, list(shape), dt, kind="ExternalOutput").ap()
    XP = di("xp", [SEQ, D]); XS = di("xs", [16, D])
    CKV = di("cache_kv", [CACHE_ROWS, 16 * KVL]); CKR = di("cache_kr", [CACHE_ROWS, 16 * ROPE])
    ST0 = di("st0", [128, 16, 128]); PT = di("pt", [128, 4], I32)
    VEC = di("vec", [128, 72]); WGU = di("wgu", [2, 128, 11, 4096]); WD = di("wd", [2, 128, 8, 2816])
    WIN = di("win", [128, 8, 4096]); WQ = di("wq", [128, 2, 3072]); WUKV = di("wukv", [128, 2048])
    WO = di("wo", [128, 2, 4096]); IDENT = di("ident", [128, 128]); MASKP = di("maskp", [128, 2048])
    MASKH = di("maskh", [64, 64]); RC = di("ropec", [64, SEQ + 16]); RS = di("ropes", [64, SEQ + 16])
    YP = do("yp", [SEQ, D]); YS = do("ys", [16, D]); KVP = do("kvp", [SEQ, KVL]); KRP = do("krp", [SEQ, ROPE])
    STP = do("stp", [4, 128, 128]); KVS = do("kvs", [16, KVL]); KRS = do("krs", [16, ROPE]); STS = do("sts", [16, 128, 128])
    DBG = do("dbg", [128, 8, WMAX]) if debug else None

    with ExitStack() as st:
        S = Sched(nc, st)
        sb = lambda name, shape, dt: st.enter_context(nc.sbuf_tensor("sb_" + name, list(shape), dt))
        PS = [st.enter_context(nc.psum_tensor(f"ps{i}", [128, 512], F32)) for i in range(8)]
        PSB = [p[:].bitcast(BF16) for p in PS]
        pk = lambda i: ('ps', i)
        hT = sb("hT", [128, 8, WMAX], F32)
        xn = sb("xn", [128, 8, WMAX], BF16)
        yb = sb("yb", [128, 8, WMAX], F32)
        scr = sb("scr", [128, 11, SW], F32)
        ring = sb("ring", [128, NSLOT, SLOT], BF16)
        sq = sb("sq", [128, 2, WMAX], BF16)
        sg = sb("sg", [128, 2, WMAX], F32)
        rstd = sb("rstd", [128, WMAX], F32)
        ckvT = sb("ckvT", [128, 2, SEQ], BF16)
        krT = sb("krT", [128, SEQ], BF16)
        ckvtok = sb("ckvtok", [128, 16, KVL], BF16)
        ckvtoks = sb("ckvtoks", [4, 4, KVL], BF16)
        Sf = sb("Sf", [128, 20, 128], F32)
        Sb = sb("Sb", [128, 20, 128], BF16)
        vec = sb("vec", [128, 72], F32)
        identf = sb("identf", [128, 128], F32)
        identb = sb("identb", [128, 128], BF16)
        onesb = sb("onesb", [128, 128], BF16)
        onesf = sb("onesf", [128, 1], F32)
        maskp = sb("maskp", [128, 2048], BF16)
        maskh = sb("maskh", [64, 64], F32)
        ropec = sb("ropec", [64, WMAX], F32)
        ropes = sb("ropes", [64, WMAX], F32)
        wukv = sb("wukv", [128, 2048], BF16)
        vtok = sb("vtok", [64, 8, 512], BF16)
        ktok = sb("ktok", [64, 8, 128], BF16)
        qn = sb("qn", [128, 2, WMAX], BF16)
        qlat = sb("qlat", [128, 4, 2, WMAX], BF16)
        qr = sb("qr", [128, 4, WMAX], BF16)
        krf = sb("krf", [64, WMAX], F32)
        pT = sb("pT", [128, 3, WMAX], BF16)
        on = sb("on", [128, 2, WMAX], BF16)
        ocat = sb("ocat", [128, 8, WMAX], BF16)
        kvst = sb("kvst", [128, 4, KVL], F32)
        krst = sb("krst", [128, 4, ROPE], F32)
        cst4 = sb("cst4", [128, 4, 3, 8], F32)
        epst = sb("epst", [128, 1], F32)
        pti = sb("pti", [128, 4], I32)
        idx = sb("idx", [128, 4], I32)
        NG = lambda n, c: vec[:, n * 8 + c: n * 8 + c + 1]
        QG = lambda c: vec[:, 48 + c: 49 + c]
        KG = lambda c: vec[:, 54 + c: 55 + c]
        HG = vec[:, 56:57]
        lbt = sb("lbt", [128, 16], F32)
        g05 = sb("g05", [128, 16], F32)

        def dma(eng, out, in_, reads, writes, key):
            return S.op(eng, lambda e: e.dma_start(out=out, in_=in_), reads=reads, writes=writes, dma=key)
        dma('sp', vec[:], VEC, [], ['vec'], 'vec')
        dma('sp', identf[:], IDENT, [], ['identf'], 'identf')
        dma('sp', maskh[:], MASKH, [], ['maskh'], 'maskh')
        dma('sp', pti[:], PT, [], ['pti'], 'pti')
        dma('sp', Sf[:, 4:20, :], ST0, [], [('Sf', i) for i in range(4, 20)], 'st0')
        dma('pool', maskp[:].rearrange("p (a b) -> p a b", a=4), MASKP.rearrange("p (a b) -> p a b", a=4), [], ['maskp', 'maskp_hi'], 'maskp')
        dma('pool', wukv[:].rearrange("p (a b) -> p a b", a=2), WUKV.rearrange("p (a b) -> p a b", a=2), [], ['wukv'], 'wukv')
        S.op('dve', lambda e: e.tensor_copy(out=identb[:], in_=identf[:]), ['identf'], ['identb'])
        S.op('dve', lambda e: e.memset(onesb[:], 1.0), [], ['onesb'])
        S.op('dve', lambda e: e.memset(onesf[:], 1.0), [], ['onesf'])
        S.op('dve', lambda e: e.memset(epst[:], EPS), [], ['epst'])
        S.op('dve', lambda e: e.memset(krT[:], 0.0), [], [('krT', i) for i in range(4)])
        S.op('dve', lambda e: e.memset(qr[:], 0.0), [], [('qr', h) for h in range(4)])
        S.op('dve', lambda e: e.memset(Sf[:, 0:4, :], 0.0), [], [('Sf', i) for i in range(4)])
        S.op('dve', lambda e: e.memset(Sb[:, 0:4, :], 0.0), [], [('Sb', i) for i in range(4)])
        S.op('act', lambda e: e.activation(out=Sb[:, 4:20, :], in_=Sf[:, 4:20, :], func=AF.Copy),
             [('Sf', i) for i in range(4, 20)], [('Sb', i) for i in range(4, 20)])
        S.op('dve', lambda e: e.tensor_tensor(out=lbt[:, 8:12], in0=vec[:, 57:61], in1=vec[:, 61:65], op=ALU.subtract), ['vec'], ['lbt'])
        S.op('act', lambda e: e.activation(out=lbt[:, 0:4], in_=lbt[:, 8:12], func=AF.Sigmoid), ['lbt'], ['lbt'])
        S.op('act', lambda e: e.activation(out=lbt[:, 4:8], in_=lbt[:, 0:4], func=AF.Identity, scale=-1.0, bias=1.0), ['lbt'], ['lbt'])
        S.op('dve', lambda e: e.tensor_scalar(out=g05[:, 0:8], in0=vec[:, 8:16], scalar1=0.5, scalar2=None, op0=ALU.mult), ['vec'], ['g05'])
        S.op('dve', lambda e: e.tensor_scalar(out=g05[:, 8:16], in0=vec[:, 40:48], scalar1=0.5, scalar2=None, op0=ALU.mult), ['vec'], ['g05'])

        rstate = {'n': 0}

        def wload(src, L):
            s = rstate['n'] % NSLOT
            rstate['n'] += 1
            h = L // 2
            dma('pool', ring[:, s, 0:L].rearrange("p (a b) -> p a b", a=2), src.rearrange("p (a b) -> p a b", a=2),
                [], [('ring', s)], ('ring', s))
            return s

        def rview(s, kc, ncol):
            return ring[:, s, 0:kc * ncol].rearrange("p (k n) -> p k n", k=kc)

        def mm(out, lhsT, rhs, start, stop, reads, bank, wkeys=None):
            S.op('pe', lambda e: e.matmul(out, lhsT=lhsT, rhs=rhs, start=start, stop=stop), reads, [pk(bank)] if wkeys is None else wkeys)

        def rms_stat(srcs, keys, Dn, W, bank=6):
            n = len(srcs)
            for c, (a, k) in enumerate(zip(srcs, keys)):
                s2 = c % 2
                S.op('act', lambda e, a=a, s2=s2: e.activation(out=sq[:a.shape[0], s2, :W], in_=a, func=AF.Square), k, [('sq', s2)])
                P = a.shape[0]
                mm(PS[bank][:, :W], onesb[:P, :], sq[:P, s2, :W], c == 0, c == n - 1, [('sq', s2), 'onesb'], bank)
            S.op('act', lambda e: e.activation(out=rstd[:, :W], in_=PS[bank][:, :W], func=AF.Ln, scale=1.0 / Dn, bias=epst[:, 0:1]),
                 [pk(bank), 'epst'], ['rstd'])
            S.op('act', lambda e: e.activation(out=rstd[:, :W], in_=rstd[:, :W], func=AF.Exp, scale=-0.5), ['rstd'], ['rstd'])

        def stt(out, in0, scalar, in1, reads, writes, op0=ALU.mult, op1=ALU.mult):
            S.op('dve', lambda e: e.scalar_tensor_tensor(out=out, in0=in0, scalar=scalar, in1=in1, op0=op0, op1=op1), reads, writes)

        def tt(out, in0, in1, op, reads, writes, eng='dve'):
            S.op(eng, lambda e: e.tensor_tensor(out=out, in0=in0, in1=in1, op=op), reads, writes)

        def act(out, in_, func, reads, writes, **kw):
            S.op('act', lambda e: e.activation(out=out, in_=in_, func=func, **kw), reads, writes)

        def norm_to_xn(gn, W):
            rms_stat([hT[:, c, :W] for c in range(8)], [[('hT', c)] for c in range(8)], D, W)
            for c in range(8):
                stt(xn[:, c, :W], hT[:, c, :W], NG(gn, c), rstd[:, :W], [('hT', c), 'rstd', 'vec'], [('xn', c)])

        def residual_from_y(gap, W):
            rms_stat([yb[:, c, :W] for c in range(8)], [[('y', c)] for c in range(8)], D, W)
            for c in range(8):
                stt(yb[:, c, :W], yb[:, c, :W], gap(c), rstd[:, :W], [('y', c), 'rstd', 'vec', 'g05'], [('y', c)])
                tt(hT[:, c, :W], hT[:, c, :W], yb[:, c, :W], ALU.add, [('hT', c), ('y', c)], [('hT', c)])

        hmv = scr[:].rearrange("p a b -> p (a b)").bitcast(BF16)

        def hm(fc, W):
            i, j = fc // 2, fc % 2
            return hmv[:, i * 2 * SW + j * WMAX: i * 2 * SW + j * WMAX + W]

        def ffn(fi, W, gpre, gpost):
            norm_to_xn(gpre, W)
            for g in range(11):
                s = wload(WGU[fi, :, g, :], 4096)
                rv = ring[:, s, :].rearrange("p (m k n) -> p m k n", m=2, k=8)
                for j in range(2):
                    fc = 2 * g + j
                    b = fc % 2
                    for kc in range(8):
                        mm(PS[b][:, :W], rv[:, 0, kc, j * 128:(j + 1) * 128], xn[:, kc, :W], kc == 0, kc == 7, [('ring', s), ('xn', kc)], b)
                    for kc in range(8):
                        mm(PS[2 + b][:, :W], rv[:, 1, kc, j * 128:(j + 1) * 128], xn[:, kc, :W], kc == 0, kc == 7, [('ring', s), ('xn', kc)], 2 + b)
                    act(sg[:, b, :W], PS[b][:, :W], AF.Silu, [pk(b)], [('sg', b)])
                    tt(hm(fc, W), sg[:, b, :W], PS[2 + b][:, :W], ALU.mult, [('sg', b), pk(2 + b)], [('scr', fc // 2)])
            for dc in range(8):
                s = wload(WD[fi, :, dc, :], 2816)
                rv = rview(s, 22, 128)
                b = 4 + dc % 2
                for fc in range(22):
                    mm(PS[b][:, :W], rv[:, fc, :], hm(fc, W), fc == 0, fc == 21, [('ring', s), ('scr', fc // 2)], b)
                act(yb[:, dc, :W], PS[b][:, :W], AF.Copy, [pk(b)], [('y', dc)])
            residual_from_y(gpost, W)

        ocatf = ocat[:].rearrange("p c w -> p (c w)").bitcast(F32)
        qlatf = qlat[:].rearrange("p h c w -> p (h c w)").bitcast(F32)

        def xstage(blk):
            if blk < 2:
                return ocatf[:, blk * 1024:(blk + 1) * 1024], [('ocat', c) for c in range(blk * 4, blk * 4 + 4)]
            return qlatf[:, (blk - 2) * 1024:(blk - 1) * 1024], [('qlat', h) for h in range((blk - 2) * 2, (blk - 2) * 2 + 2)]

        def load_x_dma(XB, nxb, src):
            for blk in range(nxb):
                ap, keys = xstage(blk)
                dma('sp', ap[:XB, :], src[blk * XB:(blk + 1) * XB, :], [], keys, ('xs', blk))

        def load_x_T(W, XB, nxb):
            for dc in range(8):
                bk = dc % 4
                for blk in range(nxb):
                    ap, keys = xstage(blk)
                    S.op('pe', lambda e, dc=dc, blk=blk, ap=ap, bk=bk: e.transpose(out=PS[bk][:, blk * XB:(blk + 1) * XB], in_=ap[:XB, dc * 128:(dc + 1) * 128],
                                                                                   identity=identf[:XB, :XB]),
                         keys + ['identf'], [pk(bk)])
                if dc % 2 == 0:
                    act(hT[:, dc, :W], PS[bk][:, :W], AF.Copy, [pk(bk)], [('hT', dc)])
                else:
                    S.op('dve', lambda e, dc=dc, bk=bk: e.tensor_copy(out=hT[:, dc, :W], in_=PS[bk][:, :W]), [pk(bk)], [('hT', dc)])

        def store_y(W, XB, nxb, dst):
            ystage = yb[:].rearrange("p c w -> p (c w)").rearrange("p (b d) -> p b d", b=4)
            for blk in range(nxb):
                for half in range(2):
                    b = 4 + (blk * 2 + half) % 4
                    for j in range(4):
                        dc = half * 4 + j
                        S.op('pe', lambda e, dc=dc, blk=blk, b=b, j=j: e.transpose(out=PS[b][:XB, j * 128:(j + 1) * 128], in_=hT[:, dc, blk * XB:(blk + 1) * XB],
                                                                                  identity=identf[:, :]),
                             [('hT', dc), 'identf'], [pk(b)])
                    if half == 0:
                        act(ystage[:XB, blk, half * 512:(half + 1) * 512], PS[b][:XB, :], AF.Copy, [pk(b)], [('y', 2 * blk + half)])
                    else:
                        S.op('dve', lambda e, blk=blk, half=half, b=b: e.tensor_copy(out=ystage[:XB, blk, half * 512:(half + 1) * 512], in_=PS[b][:XB, :]),
                             [pk(b)], [('y', 2 * blk + half)])
                dma('sp', dst[blk * XB:(blk + 1) * XB, :], ystage[:XB, blk, :], [('y', 2 * blk), ('y', 2 * blk + 1)], [], ('ys', blk))

        outtoks = []
        marks = []
        def mark(name):
            marks.append((name, len(S.ops['pe'])))

        def mixer(ti, W, C, nch, TB, tok0, sample):
            norm_to_xn(2, W)
            ctok0 = tok0 if not sample else SEQ
            dma('sp', ropec[:, :W], RC[:, ctok0:ctok0 + W], [], ['ropec'], 'ropec')
            dma('sp', ropes[:, :W], RS[:, ctok0:ctok0 + W], [], ['ropes'], 'ropes')
            XN = [('xn', k) for k in range(8)]
            def v_piece():
                s = wload(WIN[:, 0, :], 4096)
                rv = rview(s, 8, 512)

                def vmm(c, bank):
                    for kc in range(8):
                        mm(PS[bank][:C, :], xn[:, kc, c * C:(c + 1) * C], rv[:, kc, :], kc == 0, kc == 7, [('ring', s), ('xn', kc)], bank)
                nfirst = min(4, nch)
                for c in range(nfirst):
                    vmm(c, 4 + c)

                def post():
                    for c in range(nfirst):
                        act(vtok[:C, c, :], PS[4 + c][:C, :], AF.Copy, [pk(4 + c)], [('vtok', c)])
                    for c in range(nfirst, nch):
                        bank = 4 + c % 4
                        vmm(c, bank)
                        act(vtok[:C, c, :], PS[bank][:C, :], AF.Copy, [pk(bank)], [('vtok', c)])
                return post

            def mla_cq(a):
                s = wload(WIN[:, 5 + a, 0:3072], 3072)
                rv = rview(s, 8, 384)
                for j in range(3):
                    b = 4 + j
                    for kc in range(8):
                        mm(PS[b][:, :W], rv[:, kc, j * 128:(j + 1) * 128], xn[:, kc, :W], kc == 0, kc == 7, [('ring', s), ('xn', kc)], b)

                def post():
                    for j in range(3):
                        oc = a * 3 + j
                        act(yb[:, oc, :W], PS[4 + j][:, :W], AF.Copy, [pk(4 + j)], [('y', oc)])
                return post

            def mla_kv():
                s = wload(WIN[:, 7, 0:3072], 3072)
                rv = rview(s, 8, 384)
                for rc in range(2):
                    for kc in range(8):
                        mm(PS[4 + rc][:, :W], rv[:, kc, rc * 128:(rc + 1) * 128], xn[:, kc, :W], kc == 0, kc == 7, [('ring', s), ('xn', kc)], 4 + rc)
                for j in range(2):
                    for kc in range(8):
                        mm(PS[6 + j][:64, :W], rv[:, kc, 256 + j * 64:256 + (j + 1) * 64], xn[:, kc, :W], kc == 0, kc == 7, [('ring', s), ('xn', kc)], 6 + j)
                return mla_kv_post

            def mla_kv_post():
                for rc in range(2):
                    act(yb[:, 6 + rc, :W], PS[4 + rc][:, :W], AF.Copy, [pk(4 + rc)], [('y', 6 + rc)])
                tt(krf[:, :W], PS[6][:64, :W], ropec[:, :W], ALU.mult, [pk(6), 'ropec'], ['krf'])
                tt(sg[:64, 0, :W], PS[7][:64, :W], ropes[:, :W], ALU.mult, [pk(7), 'ropes'], [('sg', 0)])
                tt(krf[:, :W], krf[:, :W], sg[:64, 0, :W], ALU.add, ['krf', ('sg', 0)], ['krf'])
                if not sample:
                    act(krT[:64, tok0:tok0 + W], krf[:, :W], AF.Copy, ['krf'], [('krT', ti)])
                rms_stat([yb[:, 6 + c, :W] for c in range(2)], [[('y', 6 + c)] for c in range(2)], 256.0, W)
                for rc in range(2):
                    stt(yb[:, 6 + rc, :W], yb[:, 6 + rc, :W], KG(rc), rstd[:, :W], [('y', 6 + rc), 'rstd', 'vec'], [('y', 6 + rc)])
                    if not sample:
                        act(ckvT[:, rc, tok0:tok0 + W], yb[:, 6 + rc, :W], AF.Copy, [('y', 6 + rc)], [('ckvT', ti)])
                for blk in range(4):
                    for rc in range(2):
                        S.op('pe', lambda e, blk=blk, rc=rc: e.transpose(out=PS[4][:TB, rc * 128:(rc + 1) * 128], in_=yb[:, 6 + rc, blk * TB:(blk + 1) * TB], identity=identf[:, :]),
                             [('y', 6 + rc), 'identf'], [pk(4)])
                    S.op('pe', lambda e, blk=blk: e.transpose(out=PS[4][:TB, 256:320], in_=krf[:, blk * TB:(blk + 1) * TB], identity=identf[:64, :64]),
                         ['krf', 'identf'], [pk(4)])
                    act(kvst[:TB, blk, :], PS[4][:TB, 0:256], AF.Copy, [pk(4)], [('kvst', blk)])
                    gblk = ti * 4 + blk if not sample else blk
                    if not sample:
                        act(ckvtok[:TB, gblk, :], PS[4][:TB, 0:256], AF.Copy, [pk(4)], [('ckvtok', gblk)])
                    else:
                        act(ckvtoks[:TB, blk, :], PS[4][:TB, 0:256], AF.Copy, [pk(4)], [('ckvtoks', blk)])
                    act(krst[:TB, blk, :], PS[4][:TB, 256:320], AF.Copy, [pk(4)], [('krst', blk)])
                for blk in range(4):
                    if not sample:
                        r0 = tok0 + blk * 128
                        outtoks.append(dma('sp', KVP[r0:r0 + 128, :], kvst[:, blk, :], [('kvst', blk)], [], ('kvo', blk)))
                        outtoks.append(dma('sp', KRP[r0:r0 + 128, :], krst[:, blk, :], [('krst', blk)], [], ('kro', blk)))
                    else:
                        outtoks.append(dma('sp', KVS[blk * 4:blk * 4 + 4, :], kvst[:4, blk, :], [('kvst', blk)], [], ('kvo', blk)))
                        outtoks.append(dma('sp', KRS[blk * 4:blk * 4 + 4, :], krst[:4, blk, :], [('krst', blk)], [], ('kro', blk)))

            mla_pieces = [lambda: mla_cq(0), lambda: mla_cq(1), mla_kv, v_piece]
            T = lambda i: scr[:, i, :W]
            TK = lambda i: [('scr', i)]
            qppH = [(qn if h < 2 else on)[:, h % 2, :W] for h in range(4)]
            qppK = [[('qn' if h < 2 else 'on', h % 2)] for h in range(4)]
            ATH = [(pT[:C, h, :nch * C] if h < 3 else sq[:C, 1, :nch * C]) for h in range(4)]
            ATK = [[('pT', h)] if h < 3 else [('sq', 1)] for h in range(4)]
            ktokH = [qlat[:C, h].rearrange("p a b -> p (a b)")[:, :nch * 128].rearrange("p (c k) -> p c k", k=128) for h in range(4)]
            ktokK = [[('qlat', h)] for h in range(4)]
            gateH = [ocat[:, 4 + h, :W] for h in range(4)]
            gateK = [[('ocat', 4 + h)] for h in range(4)]
            for h in range(4):
                s = wload(WIN[:, 1 + h, 0:3072], 3072)
                rv = rview(s, 8, 384)
                for j, bnk in enumerate((0, 1, 2)):
                    for kc in range(8):
                        mm(PS[bnk][:, :W], rv[:, kc, j * 128:(j + 1) * 128], xn[:, kc, :W], kc == 0, kc == 7, [('ring', s), ('xn', kc)], bnk)
                piece_post = mla_pieces[h]()
                act(T(0), PS[0][:, :W], AF.Silu, [pk(0)], TK(0))
                act(T(1), PS[1][:, :W], AF.Sigmoid, [pk(1)], TK(1))
                act(gateH[h], PS[2][:, :W], AF.Silu, [pk(2)], gateK[h])
                S.op('dve', lambda e, h=h: e.tensor_scalar(out=T(1), in0=T(1), scalar1=lbt[:, 4 + h:5 + h], scalar2=lbt[:, h:h + 1], op0=ALU.mult, op1=ALU.add),
                     TK(1) + ['lbt'], TK(1))
                act(T(2), T(1), AF.Ln, TK(1), TK(2))
                act(T(3), T(1), AF.Identity, TK(1), TK(3), scale=-1.0, bias=1.0)
                S.op('dve', lambda e: e.memset(scr[:, 5, 0:1], 0.0), [], TK(5))
                S.op('dve', lambda e: e.tensor_tensor_scan(out=scr[:, 5, 1:W + 1], data0=bc(onesf[:, 0:1], [[0, W]]), data1=T(2), initial=0.0, op0=ALU.mult, op1=ALU.add),
                     TK(2) + ['onesf'], TK(5))
                Bv1 = bc(scr[:, 5, 1:2], [[C, nch], [1, C]])
                Bmid = bc(scr[:, 5, C // 2:C // 2 + 1], [[C, nch], [0, C]])
                Bst = bc(scr[:, 5, 0:1], [[C, nch], [0, C]])
                Bla = bc(scr[:, 5, C:C + 1], [[C, nch], [0, C]])
                T6 = bc(scr[:, 6, 0:1], [[C, nch], [1, C]])
                qp = hmv[:, 8 * 2 * SW: 8 * 2 * SW + W]
                kp = hmv[:, 8 * 2 * SW + WMAX: 8 * 2 * SW + WMAX + W]
                kppp = hmv[:, 9 * 2 * SW + WMAX: 9 * 2 * SW + WMAX + W]
                tt(T6, Bv1, Bmid, ALU.subtract, TK(5), TK(6))
                act(T(7), T(6), AF.Exp, TK(6), TK(7))
                tt(qp, T(0), T(7), ALU.mult, TK(0) + TK(7), TK(8))
                act(T(7), T(6), AF.Exp, TK(6), TK(7), scale=-1.0)
                tt(kp, T(3), T(7), ALU.mult, TK(3) + TK(7), TK(8))
                Bm1 = bc(scr[:, 5, C // 2:C // 2 + 1], [[C, nch]])
                Bs1 = bc(scr[:, 5, 0:1], [[C, nch]])
                Bl1 = bc(scr[:, 5, C:C + 1], [[C, nch]])
                tt(cst4[:, h, 0, :nch], Bm1, Bs1, ALU.subtract, TK(5), [('dec', h)])
                tt(cst4[:, h, 1, :nch], Bl1, Bm1, ALU.subtract, TK(5), [('dec', h)])
                tt(cst4[:, h, 2, :nch], Bl1, Bs1, ALU.subtract, TK(5), [('dec', h)])
                act(cst4[:, h, :, :nch], cst4[:, h, :, :nch], AF.Exp, [('dec', h)], [('dec', h)])
                tt(qppH[h].rearrange("p (c j) -> p c j", j=C), qp.rearrange("p (c j) -> p c j", j=C),
                   bc(cst4[:, h, 0, 0:1], [[1, nch], [0, C]]), ALU.mult, TK(8) + [('dec', h)], qppK[h])
                tt(kppp.rearrange("p (c j) -> p c j", j=C), kp.rearrange("p (c j) -> p c j", j=C),
                   bc(cst4[:, h, 1, 0:1], [[1, nch], [0, C]]), ALU.mult, TK(8) + [('dec', h)], TK(9))
                for c in range(nch):
                    S.op('pe', lambda e, c=c, kppp=kppp: e.transpose(out=PSB[3][:C, c * 128:(c + 1) * 128], in_=kppp[:, c * C:(c + 1) * C], identity=identb[:, :]),
                         TK(9) + ['identb'], [pk(3)])
                S.op('act', lambda e, h=h: e.activation(out=ktokH[h], in_=PSB[3][:C, :nch * 128].rearrange("p (c k) -> p c k", k=128), func=AF.Copy),
                     [pk(3)], ktokK[h])
                for c in range(nch):
                    mm(PS[0][:C, c * C:(c + 1) * C], kp[:, c * C:(c + 1) * C], qp[:, c * C:(c + 1) * C], True, True, TK(8), 0)
                tt(ATH[h].rearrange("p (c j) -> p c j", j=C), PS[0][:C, :nch * C].rearrange("p (c j) -> p c j", j=C),
                   bc(maskh[:C, 0:1], [[0, nch], [1, C]]), ALU.mult, [pk(0), 'maskh'], ATK[h])
                piece_post()
            for c in range(nch):
                for h in range(4):
                    sid = h if not sample else 4 + c * 4 + h
                    cs = slice(c * C, (c + 1) * C)
                    mm(PS[4 + h][:, cs], vtok[:C, c, h * 128:(h + 1) * 128], ATH[h][:, cs], True, False, [('vtok', c)] + ATK[h], 4 + h)
                    mm(PS[4 + h][:, cs], Sb[:, sid, :], qppH[h][:, cs], False, True, [('Sb', sid)] + qppK[h], 4 + h)
                    mm(PS[h][:, :128], ktokH[h][:, c, :], vtok[:C, c, h * 128:(h + 1) * 128], True, True, ktokK[h] + [('vtok', c)], h)
                    stt(Sf[:, sid, :], Sf[:, sid, :], cst4[:, h, 2, c:c + 1], PS[h][:, :128], [('Sf', sid), ('dec', h), pk(h)], [('Sf', sid)], op0=ALU.mult, op1=ALU.add)
                    if not sample:
                        act(Sb[:, sid, :], Sf[:, sid, :], AF.Copy, [('Sf', sid)], [('Sb', sid)])
            for h in range(4):
                rms_stat([PS[4 + h][:, :W]], [[pk(4 + h)]], 128.0, W, bank=h)
                stt(T(6), PS[4 + h][:, :W], HG, rstd[:, :W], [pk(4 + h), 'rstd', 'vec'], TK(6))
                tt(ocat[:, h, :W], T(6), gateH[h], ALU.mult, TK(6) + gateK[h], [('ocat', h)])
            mark('hgrn_end')
            if stages < 3:
                return
            rms_stat([yb[:, c, :W] for c in range(6)], [[('y', c)] for c in range(6)], 768.0, W)
            for c in range(6):
                stt(xn[:, c, :W], yb[:, c, :W], QG(c), rstd[:, :W], [('y', c), 'rstd', 'vec'] + XN, [('xn', c)])
            if stages < 3.7:
                return
            sA = wload(WQ[:, 0, :], 3072)
            rvA = rview(sA, 6, 512)
            wuk = wukv[:, 0:1024].rearrange("p (h r) -> p h r", h=4)
            wuv = wukv[:, 1024:2048].rearrange("p (h c v) -> p h c v", h=4, c=2)
            for h in range(4):
                b = h % 2
                for kc in range(6):
                    mm(PS[b][:, :W], rvA[:, kc, h * 128:(h + 1) * 128], xn[:, kc, :W], kc == 0, kc == 5, [('ring', sA), ('xn', kc)], b)
                act(qn[:, b, :W], PS[b][:, :W], AF.Copy, [pk(b)], [('qn', b)])
                for rc in range(2):
                    mm(PS[2 + rc][:, :W], wuk[:, h, rc * 128:(rc + 1) * 128], qn[:, b, :W], True, True, [('qn', b), 'wukv'], 2 + rc)
                    act(qlat[:, h, rc, :W], PS[2 + rc][:, :W], AF.Copy, [pk(2 + rc)], [('qlat', h)])
            sB = wload(WQ[:, 1, :], 3072)
            rvB = rview(sB, 6, 512)
            for h in range(4):
                for j in range(2):
                    for kc in range(6):
                        mm(PS[4 + j][:64, :W], rvB[:, kc, j * 256 + h * 64: j * 256 + (h + 1) * 64], xn[:, kc, :W], kc == 0, kc == 5, [('ring', sB), ('xn', kc)], 4 + j)
                tt(scr[:64, 6, :W], PS[4][:64, :W], ropec[:, :W], ALU.mult, [pk(4), 'ropec'], TK(6))
                tt(scr[:64, 7, :W], PS[5][:64, :W], ropes[:, :W], ALU.mult, [pk(5), 'ropes'], TK(7))
                tt(qr[:64, h, :W], scr[:64, 6, :W], scr[:64, 7, :W], ALU.add, TK(6) + TK(7), [('qr', h)])
            mark('proj_end')
            if stages < 4:
                return
            if not sample:
                nkb = 4 * (ti + 1)
                steps = [(h, j) for h in range(4) for j in range(nkb)]

                def scores(n):
                    h, j = steps[n]
                    b = n % 2
                    ks = slice(j * 128, (j + 1) * 128)
                    kti = j // 4
                    c0 = max(0, j - 4 * ti) * 128
                    cs = slice(c0, W)
                    mm(PS[b][:, cs], ckvT[:, 0, ks], qlat[:, h, 0, cs], True, False, [('ckvT', kti), ('qlat', h)], b)
                    mm(PS[b][:, cs], ckvT[:, 1, ks], qlat[:, h, 1, cs], False, False, [('ckvT', kti), ('qlat', h)], b)
                    mm(PS[b][:, cs], krT[:, ks], qr[:, h, cs], False, True, [('krT', kti), ('qr', h)], b)
                    pb = n % 3
                    act(pT[:, pb, cs], PS[b][:, cs], AF.Exp, [pk(b)], [('pT', pb)], scale=SCALE)
                    if j >= 4 * ti:
                        dsl = slice(c0, c0 + 128)
                        tt(pT[:, pb, dsl], pT[:, pb, dsl], maskp[:, 0:128], ALU.mult, [('pT', pb), 'maskp'], [('pT', pb)])

                def pv(n):
                    h, j = steps[n]
                    pb = n % 3
                    c0 = max(0, j - 4 * ti) * 128
                    cs = slice(c0, W)
                    for rc in range(2):
                        mm(PS[2 + rc][:, cs], ckvtok[:, j, rc * 128:(rc + 1) * 128], pT[:, pb, cs], j == 0, j == nkb - 1, [('ckvtok', j), ('pT', pb)], 2 + rc)
                    mm(PS[4][:, cs], onesb[:, :], pT[:, pb, cs], j == 0, j == nkb - 1, ['onesb', ('pT', pb)], 4)
                    if j == nkb - 1:
                        act(rstd[:, :W], PS[4][:, :W], AF.Ln, [pk(4)], ['rstd'])
                        act(rstd[:, :W], rstd[:, :W], AF.Exp, ['rstd'], ['rstd'], scale=-1.0)
                        for rc in range(2):
                            tt(on[:, rc, :W], PS[2 + rc][:, :W], rstd[:, :W], ALU.mult, [pk(2 + rc), 'rstd'], [('on', rc)])
                        for rc in range(2):
                            mm(PS[5][:, :W], wuv[:, h, rc, :], on[:, rc, :W], rc == 0, rc == 1, [('on', rc), 'wukv'], 5)
                        act(ocat[:, 4 + h, :W], PS[5][:, :W], AF.Copy, [pk(5)], [('ocat', 4 + h)])

                scores(0)
                for n in range(len(steps)):
                    if n + 1 < len(steps):
                        scores(n + 1)
                    pv(n)
            else:
                sample_attention(W, wuv)

        def scrk(lo, hi):
            return [('scr', i) for i in range(lo // (2 * SW), (hi - 1) // (2 * SW) + 1)]

        def sample_attention(W, wuv):
            ybf = yb[:].rearrange("p c w -> p (c w)").bitcast(BF16)
            xnf = xn[:].rearrange("p c w -> p (c w)")
            for rc in range(2):
                act(on[:, rc, :W], yb[:, 6 + rc, :W], AF.Copy, [('y', 6 + rc)], [('on', rc)])
            act(pT[:64, 2, :W], krf[:, :W], AF.Copy, ['krf'], [('pT', 2)])
            GR0 = 8192
            PN0 = 11264
            ONB0 = 11328
            seq = [(b, tg) for b in range(4) for tg in range(8)]

            ckvT_flat = ckvT[:].rearrange("p c s -> p (c s)")
            ckvtok_flat = ckvtok[:].rearrange("p b d -> p (b d)")

            def bufs(n):
                b, tg = seq[n]
                g3 = n % 4
                gb = n % 2
                if g3 < 2:
                    kgk = scrk(g3 * 4096, (g3 + 1) * 4096)
                    gk = hmv[:, g3 * 4096:(g3 + 1) * 4096]
                elif g3 == 2:
                    kgk = [('ckvT', i) for i in range(4)]
                    gk = ckvT_flat
                else:
                    kgk = [('ckvtok', i) for i in range(16)]
                    gk = ckvtok_flat
                if g3 < 3:
                    kgr = scrk(GR0 + g3 * 1024, GR0 + (g3 + 1) * 1024)
                    gr = hmv[:, GR0 + g3 * 1024: GR0 + (g3 + 1) * 1024]
                else:
                    kgr = ['maskp_hi']
                    gr = maskp[:, 1024:2048]
                kkT = [('y', c) for c in range(gb * 4, gb * 4 + 4)]
                krT_ = [('xn', c) for c in range(gb * 4, gb * 4 + 4)]
                gk3 = gk.rearrange("p (t d) -> p t d", d=256)
                gr3 = gr.rearrange("p (t d) -> p t d", d=64)
                kTb = ybf[:, gb * 4096:(gb + 1) * 4096].rearrange("p (t c k) -> p t c k", t=16, c=2)
                rTb = xnf[:64, gb * 2048:(gb + 1) * 2048].rearrange("p (t k) -> p t k", t=16)
                return b, tg, g3, kgk, kgr, kkT, krT_, gk, gr, gk3, gr3, kTb, rTb

            def Gstage(n):
                b, tg, g3, kgk, kgr, kkT, krT_, gk, gr, gk3, gr3, kTb, rTb = bufs(n)
                S.op('dve', lambda e: e.tensor_scalar(out=idx[:, g3:g3 + 1], in0=pti[:, b:b + 1], scalar1=8, scalar2=tg, op0=ALU.mult, op1=ALU.add),
                     ['pti'], [('idx', g3)])
                S.op('pool', lambda e: e.indirect_dma_start(out=gk, out_offset=None, in_=CKV, in_offset=bass.IndirectOffsetOnAxis(ap=idx[:, g3:g3 + 1], axis=0)),
                     [('idx', g3)], kgk, dma=('gk', g3))
                S.op('pool', lambda e: e.indirect_dma_start(out=gr, out_offset=None, in_=CKR, in_offset=bass.IndirectOffsetOnAxis(ap=idx[:, g3:g3 + 1], axis=0)),
                     [('idx', g3)], kgr, dma=('gr', g3))

            def Tstage(n):
                b, tg, g3, kgk, kgr, kkT, krT_, gk, gr, gk3, gr3, kTb, rTb = bufs(n)
                for q4 in range(4):
                    tb = q4 % 2
                    for t4 in range(4):
                        t = q4 * 4 + t4
                        for rc in range(2):
                            S.op('pe', lambda e, t=t, rc=rc, t4=t4, tb=tb: e.transpose(out=PSB[tb][:, (t4 * 2 + rc) * 128:(t4 * 2 + rc + 1) * 128],
                                                                               in_=gk3[:, t, rc * 128:(rc + 1) * 128], identity=identb[:, :]),
                                 kgk + ['identb'], [pk(tb)])
                    src4 = PSB[tb][:, :].rearrange("p (t c k) -> p t c k", t=4, c=2)
                    if q4 % 2 == 0:
                        S.op('act', lambda e, q4=q4, src4=src4: e.activation(out=kTb[:, q4 * 4:(q4 + 1) * 4, :, :], in_=src4, func=AF.Copy), [pk(tb)], kkT)
                    else:
                        S.op('dve', lambda e, q4=q4, src4=src4: e.tensor_copy(out=kTb[:, q4 * 4:(q4 + 1) * 4, :, :], in_=src4), [pk(tb)], kkT)
                for q8 in range(2):
                    rbk = 7 if q8 == 0 else 3
                    for t8 in range(8):
                        t = q8 * 8 + t8
                        S.op('pe', lambda e, t=t, t8=t8, rbk=rbk: e.transpose(out=PSB[rbk][:64, t8 * 128:(t8 + 1) * 128], in_=gr3[:, t, :], identity=identb[:, :]),
                             kgr + ['identb'], [pk(rbk)])
                    src8 = PSB[rbk][:64, :].rearrange("p (t k) -> p t k", t=8)
                    if q8 == 0:
                        S.op('dve', lambda e, q8=q8, src8=src8: e.tensor_copy(out=rTb[:, q8 * 8:(q8 + 1) * 8, :], in_=src8), [pk(rbk)], krT_)
                    else:
                        S.op('act', lambda e, q8=q8, src8=src8: e.activation(out=rTb[:, q8 * 8:(q8 + 1) * 8, :], in_=src8, func=AF.Copy), [pk(rbk)], krT_)

            def Sstage(n):
                b, tg, gb, kgk, kgr, kkT, krT_, gk, gr, gk3, gr3, kTb, rTb = bufs(n)
                qs = slice(4 * b, 4 * b + 4)
                half = n % 2
                gb2 = n % 2
                wk = [('ps2h', half)] + ([pk(2)] if n < 2 else [])
                for t in range(16):
                    o3 = PS[2][:, half * 256 + t * 16: half * 256 + (t + 1) * 16].rearrange("p (h t) -> p h t", h=4)
                    mm(o3, kTb[:, t, 0, :], qlat[:, :, 0, qs], True, False, kkT + [('qlat', h) for h in range(4)], 2, wk)
                    mm(o3, kTb[:, t, 1, :], qlat[:, :, 1, qs], False, False, kkT, 2, wk)
                    mm(o3, xnf[:, gb2 * 2048:(gb2 + 1) * 2048].rearrange("p (t k) -> p t k", t=16)[:, t, :], qr[:, :, qs], False, True, krT_ + [('qr', h) for h in range(4)], 2, wk)
                pb = n % 2
                act(pT[:, pb, 0:256], PS[2][:, half * 256: half * 256 + 256], AF.Exp, [('ps2h', half)] + ([pk(2)] if n >= len(seq) - 2 else []), [('pT', pb)], scale=SCALE)

            def Vstage(n):
                b, tg, gb, kgk, kgr, kkT, krT_, gk, gr, gk3, gr3, kTb, rTb = bufs(n)
                qs = slice(4 * b, 4 * b + 4)
                pb = n % 2
                for t in range(16):
                    pc = pT[:, pb, t * 16:(t + 1) * 16]
                    st0_ = (tg == 0 and t == 0)
                    for rc in range(2):
                        mm(PS[4 + rc][:, 0:16], gk3[:, t, rc * 128:(rc + 1) * 128], pc, st0_, False, kgk + [('pT', pb)], 4 + rc)
                    mm(PS[6][:, 0:16], onesb[:, :], pc, st0_, False, ['onesb', ('pT', pb)], 6)
                if tg != 7:
                    return
                o3 = PS[7][:4, 0:16].rearrange("p (h t) -> p h t", h=4)
                mm(o3, on[:, 0, qs], qlat[:, :, 0, qs], True, False, [('on', 0)], 7)
                mm(o3, on[:, 1, qs], qlat[:, :, 1, qs], False, False, [('on', 1)], 7)
                mm(o3, pT[:, 2, qs], qr[:, :, qs], False, True, [('pT', 2)], 7)
                act(sg[:4, 0, 0:16], PS[7][:4, 0:16], AF.Exp, [pk(7)], [('sg', 0)], scale=SCALE)
                kpn = scrk(PN0, PN0 + 16)
                pn = hmv[:4, PN0:PN0 + 16]
                tt(pn.rearrange("p (h t) -> p h t", h=4), sg[:4, 0, 0:16].rearrange("p (h t) -> p h t", h=4), bc(maskh[:4, 0:1], [[0, 4], [1, 4]]), ALU.mult,
                   [('sg', 0), 'maskh'], kpn)
                for rc in range(2):
                    mm(PS[4 + rc][:, 0:16], ckvtoks[:4, b, rc * 128:(rc + 1) * 128], pn, False, True, [('ckvtoks', b)] + kpn, 4 + rc)
                mm(PS[6][:, 0:16], onesb[:4, :], pn, False, True, ['onesb'] + kpn, 6)
                act(rstd[:, 0:16], PS[6][:, 0:16], AF.Ln, [pk(6)], ['rstd'])
                act(rstd[:, 0:16], rstd[:, 0:16], AF.Exp, ['rstd'], ['rstd'], scale=-1.0)
                konb = scrk(ONB0, ONB0 + 32)
                onb = hmv[:, ONB0:ONB0 + 32].rearrange("p (c n) -> p c n", c=2)
                for rc in range(2):
                    tt(onb[:, rc, :], PS[4 + rc][:, 0:16], rstd[:, 0:16], ALU.mult, [pk(4 + rc), 'rstd'], konb)
                for h in range(4):
                    for rc in range(2):
                        mm(PS[7][:, 32 + h * 4: 32 + h * 4 + 4], wuv[:, h, rc, :], onb[:, rc, h * 4:(h + 1) * 4], rc == 0, rc == 1, konb + ['wukv'], 7)
                for h in range(4):
                    act(ocat[:, 4 + h, qs], PS[7][:, 32 + h * 4:32 + h * 4 + 4], AF.Copy, [pk(7)], [('ocat', 4 + h)])

            Gstage(0)
            Gstage(1)
            Gstage(2)
            Tstage(0)
            for n in range(len(seq)):
                Sstage(n)
                if n + 3 < len(seq):
                    Gstage(n + 3)
                if n + 1 < len(seq):
                    Tstage(n + 1)
                Vstage(n)

        def wout(W):
            for g in range(2):
                s = wload(WO[:, g, :], 4096)
                rv = rview(s, 8, 512)
                for j in range(4):
                    dc = g * 4 + j
                    b = dc % 2
                    for c in range(8):
                        mm(PS[b][:, :W], rv[:, c, j * 128:(j + 1) * 128], ocat[:, c, :W], c == 0, c == 7, [('ring', s), ('ocat', c)], b)
                    act(yb[:, dc, :W], PS[b][:, :W], AF.Copy, [pk(b)], [('y', dc)])
            residual_from_y(lambda c: NG(3, c), W)

        tiles = [(i, WMAX, 64, 8, 128, i * WMAX, False) for i in range(4)] + [(4, 16, 4, 4, 4, 0, True)]
        for (ti, W, C, nch, TB, tok0, sample) in tiles:
            if ti == 0:
                load_x_dma(128, 4, XP[0:WMAX, :])
            load_x_T(W, 128 if not sample else 16, 4 if not sample else 1)
            mark('tile%d_start' % ti)
            if stages >= 1:
                ffn(0, W, 0, lambda c: g05[:, c:c + 1])
            mark('ffn1_end')
            if stages >= 2:
                mixer(ti, W, C, nch, TB, tok0, sample)
            mark('attn_end')
            if stages >= 5:
                wout(W)
            mark('wout_end')
            if ti < 3:
                load_x_dma(128, 4, XP[(ti + 1) * WMAX:(ti + 2) * WMAX, :])
            elif ti == 3:
                load_x_dma(16, 1, XS)
            if stages >= 6:
                ffn(1, W, 4, lambda c: g05[:, 8 + c:9 + c])
            mark('ffn2_end')
            if not sample:
                store_y(W, 128, 4, YP[tok0:tok0 + W, :])
            else:
                store_y(W, 16, 1, YS)
            if not sample and ti == 3 and stages >= 2:
                outtoks.append(dma('sp', STP.rearrange("h k v -> k h v"), Sf[:, 0:4, :], [('Sf', i) for i in range(4)], [], 'stp'))
            if sample and stages >= 2:
                outtoks.append(dma('sp', STS.rearrange("s k v -> k s v"), Sf[:, 4:20, :], [('Sf', i) for i in range(4, 20)], [], 'sts'))

        toks = list(outtoks)
        for key, sem in S.dma_sems.items():
            toks.append((sem, S.dma_cum[key], 'dma', ('d', key)))
        S.wait_tokens('sp', toks)
        S.emit()
    nc._marks = marks
    return nc


def _consts():
    ident = np.eye(128, dtype=np.float32)
    s = np.arange(128)[:, None]
    q = np.arange(512)[None, :]
    maskp = np.concatenate([(o * 128 + s <= q).astype(np.float32) for o in range(4)], axis=1)
    maskh = (np.arange(64)[:, None] <= np.arange(64)[None, :]).astype(np.float32)
    half = 32
    inv = (np.float64(10000.0) ** (-np.arange(half, dtype=np.float64) / half)).astype(np.float32)
    pos = np.concatenate([np.arange(SEQ), np.tile(16384 + np.arange(4), 4)]).astype(np.float32)
    ang = (pos[None, :] * inv[:, None]).astype(np.float32).astype(np.float64)
    cos = np.cos(ang).astype(np.float32)
    sin = np.sin(ang).astype(np.float32)
    ropec = np.concatenate([cos, cos], axis=0)
    ropes = np.concatenate([-sin, sin], axis=0)
    return ident, maskp, maskh, np.ascontiguousarray(ropec), np.ascontiguousarray(ropes)


def _layout_weights(inp):
    f = lambda a: np.ascontiguousarray(a, dtype=np.float32)
    wg, wu, wd = inp["w_ffn_gate"][0], inp["w_ffn_up"][0], inp["w_ffn_down"][0]
    wgu = np.empty((2, 128, 11, 2, 8, 256), np.float32)
    wdl = np.empty((2, 128, 8, 22, 128), np.float32)
    for fi in range(2):
        a = wg[fi].reshape(8, 128, 11, 256).transpose(1, 2, 0, 3)
        b = wu[fi].reshape(8, 128, 11, 256).transpose(1, 2, 0, 3)
        wgu[fi, :, :, 0] = a
        wgu[fi, :, :, 1] = b
        wdl[fi] = wd[fi].reshape(22, 128, 8, 128).transpose(1, 2, 0, 3)
    wgu = wgu.reshape(2, 128, 11, 4096)
    wdl = wdl.reshape(2, 128, 8, 2816)
    win = inp["w_in"][0]
    def grp(cols):
        a = win[:, cols].reshape(8, 128, len(cols)).transpose(1, 0, 2)
        out = np.zeros((128, 4096), np.float32)
        out[:, :8 * len(cols)] = a.reshape(128, -1)
        return out
    ar = np.arange
    groups = [grp(ar(1024, 1536))]
    for h in range(4):
        groups.append(grp(np.concatenate([ar(h * 128, (h + 1) * 128), 512 + ar(h * 128, (h + 1) * 128), 1536 + ar(h * 128, (h + 1) * 128)])))
    groups.append(grp(ar(2048, 2048 + 384)))
    groups.append(grp(ar(2048 + 384, 2816)))
    sw = np.concatenate([ar(32, 64), ar(0, 32)])
    groups.append(grp(np.concatenate([ar(2816, 3072), 3072 + ar(64), 3072 + sw])))
    winl = np.stack(groups, axis=1)
    wq = inp["w_q_up"][0].reshape(768, 768)
    colsA = np.concatenate([h * 192 + ar(128) for h in range(4)])
    colsB = np.concatenate([h * 192 + 128 + ar(64) for h in range(4)] + [h * 192 + 128 + sw for h in range(4)])
    def grq(cols):
        return wq[:, cols].reshape(6, 128, 512).transpose(1, 0, 2).reshape(128, 3072)
    wql = np.stack([grq(colsA), grq(colsB)], axis=1)
    wkv = inp["w_kv_up"][0]
    wuk = wkv[:, :, :128].transpose(2, 1, 0).reshape(128, 1024)
    wuv = wkv[:, :, 128:].reshape(2, 128, 4, 128).transpose(1, 2, 0, 3).reshape(128, 1024)
    wukv = np.concatenate([wuk, wuv], axis=1)
    wo = inp["w_out"][0].reshape(8, 128, 2, 512).transpose(1, 2, 0, 3).reshape(128, 2, 4096)
    vec = np.zeros((128, 72), np.float32)
    vec[:, 0:48] = inp["norm_g"][0].reshape(6, 8, 128).transpose(2, 0, 1).reshape(128, 48)
    vec[:, 48:54] = inp["q_norm_g"][0].reshape(6, 128).T
    vec[:, 54:56] = inp["kv_norm_g"][0].reshape(2, 128).T
    vec[:, 56] = inp["hg_norm_g"][0]
    lbl = inp["hgrn_lb_logits"]
    vec[:, 57:61] = lbl[0].reshape(4, 128).T
    vec[:, 61:65] = lbl[1].reshape(4, 128).T
    return f(wgu), f(wdl), f(winl), f(wql), f(wukv), f(wo), vec


_CACHE = {}


def kernel(**inp):
    inp = {k: np.asarray(v) for k, v in inp.items()}
    ident, maskp, maskh, ropec, ropes = _consts()
    wgu, wdl, winl, wql, wukv, wo, vec2 = _layout_weights(inp)
    nc = _CACHE.get('nc')
    if nc is None:
        nc = build()
        _CACHE['nc'] = nc
    ckv = np.ascontiguousarray(inp["cache_kv_latent"][0]).reshape(5120 * 8, 16 * KVL)[:CACHE_ROWS]
    ckr = np.ascontiguousarray(inp["cache_k_rope"][0]).reshape(5120 * 8, 16 * ROPE)[:CACHE_ROWS]
    in_maps = []
    for c in range(NCORES):
        pt = np.ascontiguousarray(inp["page_table"][4 * c:4 * c + 4].T.astype(np.int32))
        st0 = np.ascontiguousarray(inp["state_hgrn"][0, 4 * c:4 * c + 4].reshape(16, 128, 128).transpose(1, 0, 2))
        in_maps.append({
            "xp": np.ascontiguousarray(inp["x_prompt"][c]), "xs": np.ascontiguousarray(inp["x_sample"][4 * c:4 * c + 4].reshape(16, D)),
            "cache_kv": ckv, "cache_kr": ckr, "st0": st0, "pt": pt, "vec": vec2, "wgu": wgu, "wd": wdl, "win": winl, "wq": wql,
            "wukv": wukv, "wo": wo, "ident": ident, "maskp": maskp, "maskh": maskh, "ropec": ropec, "ropes": ropes,
        })
    res = run_bass_kernel_spmd(nc, in_maps, core_ids=list(range(NCORES)))
    R = res.results
    y_p = np.stack([R[c]["yp"] for c in range(NCORES)])
    y_s = np.concatenate([R[c]["ys"].reshape(4, 4, D) for c in range(NCORES)])
    kv_p = np.stack([R[c]["kvp"] for c in range(NCORES)])[None]
    kr_p = np.stack([R[c]["krp"] for c in range(NCORES)])[None]
    st_p = np.stack([R[c]["stp"] for c in range(NCORES)])[None]
    kv_s = np.concatenate([R[c]["kvs"].reshape(4, 4, KVL) for c in range(NCORES)])[None]
    kr_s = np.concatenate([R[c]["krs"].reshape(4, 4, ROPE) for c in range(NCORES)])[None]
    st_s = np.concatenate([R[c]["sts"].reshape(4, 4, 128, 128) for c in range(NCORES)])[None]
    return (y_p, y_s, kv_p, kr_p, st_p, kv_s, kr_s, st_s)
```
